# Optimizing a Trainium2 kernel written in Bass

```python
import math
import jax, jax.numpy as jnp
from jax import lax
import numpy as np

D_MODEL = 1024
BATCH = 16
SEQ = 2048
DEPTH = 1

HEAD_DIM = 64
NSA_HEADS = 8
NSA_KV_GROUPS = 2
NSA_Q_PER_GROUP = NSA_HEADS // NSA_KV_GROUPS
CMP_BLOCK = 32
CMP_STRIDE = 16
CMP_HIDDEN = 128
SLC_BLOCK = 64
SLC_TOPK = 8
WINDOW = 512
FOX_HEADS = 8
Q_BLOCK = 128
D_FF = 2816
N_BUCKETS = 32
MAX_DISTANCE = 128
RMS_EPS = 1e-6
NEG_INF = -1.0e30
FORCE_BONUS = 1.0e4

NSA_W = NSA_HEADS * HEAD_DIM
NSA_KV_W = NSA_KV_GROUPS * HEAD_DIM
FOX_W = FOX_HEADS * HEAD_DIM
IN_SPLITS = (NSA_W, NSA_KV_W, NSA_KV_W, NSA_KV_W, NSA_KV_W, NSA_KV_W, NSA_KV_W, 3 * NSA_HEADS,
             FOX_W, FOX_W, FOX_W, FOX_HEADS, 2 * D_MODEL)
D_IN = sum(IN_SPLITS)

kernel_name = 'hybrid_nsa_fox_macaron'


def _rms(x, g):
    xf = x.astype(jnp.float32)
    y = xf * lax.rsqrt(jnp.mean(xf * xf, axis=-1, keepdims=True) + RMS_EPS)
    return (y * g.astype(jnp.float32)).astype(x.dtype)


def _swiglu(x, w_up, w_down):
    gate, up = jnp.split(x @ w_up, 2, axis=-1)
    return (jax.nn.silu(gate) * up) @ w_down


def _t5_bucket(dist):
    n = jnp.maximum(dist, 0)
    max_exact = N_BUCKETS // 2
    nf = jnp.maximum(n, 1).astype(jnp.float32)
    large = max_exact + (jnp.log(nf / max_exact) / math.log(MAX_DISTANCE / max_exact)
                         * (N_BUCKETS - max_exact)).astype(jnp.int32)
    large = jnp.minimum(large, N_BUCKETS - 1)
    return jnp.where(n < max_exact, n, large)


def _masked_softmax(logits, mask):
    p = jax.nn.softmax(jnp.where(mask, logits, NEG_INF), axis=-1)
    return p * mask


def _compress(kv, pos, w1, w2):
    B, S, G, dh = kv.shape
    nc = (S - CMP_BLOCK) // CMP_STRIDE + 1
    idx = np.arange(nc)[:, None] * CMP_STRIDE + np.arange(CMP_BLOCK)[None, :]
    blocks = kv[:, idx] + pos[None, None, :, None, :]
    blocks = blocks.transpose(0, 3, 1, 2, 4).reshape(B, G, nc, CMP_BLOCK * dh)
    return jax.nn.silu(blocks @ w1) @ w2


def _hybrid_layer(x, ffn1_norm, ffn1_w_up, ffn1_w_down, mix_norm, w_in, b_forget, nsa_q_gain, nsa_k_gain,
                  fox_q_gain, fox_k_gain, cmp_pos_k, cmp_pos_v, cmp_k_w1, cmp_k_w2, cmp_v_w1, cmp_v_w2,
                  w_o_nsa, w_o_fox, w_out, ffn2_norm, ffn2_w_up, ffn2_w_down, rel_bias_table):
    B, S, D = x.shape
    G, R, H, dh, HB = NSA_KV_GROUPS, NSA_Q_PER_GROUP, NSA_HEADS, HEAD_DIM, FOX_HEADS
    nq = S // Q_BLOCK
    scale = 1.0 / math.sqrt(dh)
    f32 = jnp.float32
    t_pos = jnp.arange(S)
    starts = jnp.arange(nq) * Q_BLOCK

    x = x + 0.5 * _swiglu(_rms(x, ffn1_norm), ffn1_w_up, ffn1_w_down)

    u = _rms(x, mix_norm)
    split_pts = np.cumsum(np.array(IN_SPLITS))[:-1].tolist()
    qa, kc, vc, ks, vs, kw, vw, ga, qb, kb, vb, fb, gm = jnp.split(u @ w_in, split_pts, axis=-1)

    qa = _rms(qa.reshape(B, S, H, dh), nsa_q_gain) * scale
    qa = qa.reshape(B, S, G, R, dh).transpose(0, 2, 3, 1, 4)
    kc = _rms(_compress(kc.reshape(B, S, G, dh), cmp_pos_k, cmp_k_w1, cmp_k_w2), nsa_k_gain[0])
    vc = _compress(vc.reshape(B, S, G, dh), cmp_pos_v, cmp_v_w1, cmp_v_w2)
    ks = _rms(ks.reshape(B, S, G, dh), nsa_k_gain[1]).transpose(0, 2, 1, 3)
    vs = vs.reshape(B, S, G, dh).transpose(0, 2, 1, 3)
    kw = _rms(kw.reshape(B, S, G, dh), nsa_k_gain[2]).transpose(0, 2, 1, 3)
    vw = vw.reshape(B, S, G, dh).transpose(0, 2, 1, 3)

    dist_bias = rel_bias_table[_t5_bucket(t_pos)].T.reshape(G, R, S).astype(f32)

    nc = kc.shape[2]
    cmp_end = jnp.arange(nc) * CMP_STRIDE + CMP_BLOCK - 1
    dist_c = t_pos[:, None] - cmp_end[None, :]
    logit_c = (jnp.einsum('bgrsd,bgcd->bgrsc', qa, kc).astype(f32)
               + dist_bias[:, :, jnp.clip(dist_c, 0, S - 1)])
    p_c = _masked_softmax(logit_c, dist_c >= 0)
    o_cmp = jnp.einsum('bgrsc,bgcd->bgrsd', p_c.astype(vc.dtype), vc)

    ns = S // SLC_BLOCK
    ci = np.arange(nc)[:, None] * CMP_STRIDE
    sj = np.arange(ns)[None, :] * SLC_BLOCK
    overlap = ((ci <= sj + SLC_BLOCK - 1) & (ci + CMP_BLOCK - 1 >= sj)).astype(np.float32)
    imp = jnp.einsum('bgrsc,cj->bgsj', p_c, jnp.asarray(overlap))
    cur = t_pos // SLC_BLOCK
    blk = jnp.arange(ns)
    forced = (blk[None, :] == 0) | (blk[None, :] == cur[:, None]) | (blk[None, :] == cur[:, None] - 1)
    imp = jnp.where(blk[None, :] <= cur[:, None], imp + FORCE_BONUS * forced, NEG_INF)
    n_sel = min(SLC_TOPK, ns)
    _, sel_idx = lax.top_k(imp, n_sel)

    ks_blk = ks.reshape(B, G, ns, SLC_BLOCK, dh)
    vs_blk = vs.reshape(B, G, ns, SLC_BLOCK, dh)
    kw_pad = jnp.pad(kw, ((0, 0), (0, 0), (WINDOW, 0), (0, 0)))
    vw_pad = jnp.pad(vw, ((0, 0), (0, 0), (WINDOW, 0), (0, 0)))
    q_chunks = qa.reshape(B, G, R, nq, Q_BLOCK, dh).transpose(3, 0, 1, 2, 4, 5)
    idx_chunks = sel_idx.reshape(B, G, nq, Q_BLOCK, n_sel).transpose(2, 0, 1, 3, 4)
    bi = jnp.arange(B)[:, None, None, None]
    gi = jnp.arange(G)[None, :, None, None]
    n_keys = n_sel * SLC_BLOCK

    def nsa_block(args):
        q, idx, start = args
        t = start + jnp.arange(Q_BLOCK)
        k_sel = ks_blk[bi, gi, idx]
        v_sel = vs_blk[bi, gi, idx]
        tok = idx[..., None] * SLC_BLOCK + jnp.arange(SLC_BLOCK)
        dist = t[None, None, :, None, None] - tok
        bias = jax.vmap(lambda tab, d: tab[:, d], in_axes=(0, 1), out_axes=0)(
            dist_bias, jnp.clip(dist, 0, S - 1))
        bias = bias.transpose(2, 0, 1, 3, 4, 5)
        logit = jnp.einsum('bgrtd,bgtnld->bgrtnl', q, k_sel).astype(f32) + bias
        logit = logit.reshape(B, G, R, Q_BLOCK, n_keys)
        mask = (dist >= 0).reshape(B, G, 1, Q_BLOCK, n_keys)
        p = _masked_softmax(logit, mask)
        o_s = jnp.einsum('bgrtk,bgtkd->bgrtd', p.astype(v_sel.dtype),
                         v_sel.reshape(B, G, Q_BLOCK, n_keys, dh))
        k_win = lax.dynamic_slice_in_dim(kw_pad, start, WINDOW + Q_BLOCK, axis=2)
        v_win = lax.dynamic_slice_in_dim(vw_pad, start, WINDOW + Q_BLOCK, axis=2)
        s_pos = start - WINDOW + jnp.arange(WINDOW + Q_BLOCK)
        dist_w = t[:, None] - s_pos[None, :]
        mask_w = (dist_w >= 0) & (dist_w < WINDOW) & (s_pos[None, :] >= 0)
        logit_w = (jnp.einsum('bgrtd,bgsd->bgrts', q, k_win).astype(f32)
                   + dist_bias[:, :, jnp.clip(dist_w, 0, S - 1)])
        p_w = _masked_softmax(logit_w, mask_w)
        o_w = jnp.einsum('bgrts,bgsd->bgrtd', p_w.astype(v_win.dtype), v_win)
        return o_s, o_w

    o_slc, o_win = lax.map(nsa_block, (q_chunks, idx_chunks, starts))
    o_slc = o_slc.transpose(1, 0, 4, 2, 3, 5).reshape(B, S, H, dh)
    o_win = o_win.transpose(1, 0, 4, 2, 3, 5).reshape(B, S, H, dh)
    o_cmp = o_cmp.transpose(0, 3, 1, 2, 4).reshape(B, S, H, dh)
    g_nsa = jax.nn.sigmoid(ga).reshape(B, S, 3, H)[..., None]
    o_nsa = (g_nsa[:, :, 0] * o_cmp + g_nsa[:, :, 1] * o_slc + g_nsa[:, :, 2] * o_win).reshape(B, S, NSA_W)

    qb = (_rms(qb.reshape(B, S, HB, dh), fox_q_gain) * scale).transpose(0, 2, 1, 3)
    kb = _rms(kb.reshape(B, S, HB, dh), fox_k_gain).transpose(0, 2, 1, 3)
    vb = vb.reshape(B, S, HB, dh).transpose(0, 2, 1, 3)
    log_f = jax.nn.log_sigmoid((fb + b_forget).astype(f32))
    cum = jnp.cumsum(log_f, axis=1).transpose(0, 2, 1)
    qb_chunks = qb.reshape(B, HB, nq, Q_BLOCK, dh).transpose(2, 0, 1, 3, 4)
    cum_chunks = cum.reshape(B, HB, nq, Q_BLOCK).transpose(2, 0, 1, 3)

    def fox_block(args):
        q, cq, start = args
        t = start + jnp.arange(Q_BLOCK)
        logit = (jnp.einsum('bhtd,bhsd->bhts', q, kb).astype(f32)
                 + cq[..., None] - cum[:, :, None, :])
        p = _masked_softmax(logit, t_pos[None, :] <= t[:, None])
        return jnp.einsum('bhts,bhsd->bhtd', p.astype(vb.dtype), vb)

    o_fox = lax.map(fox_block, (qb_chunks, cum_chunks, starts))
    o_fox = o_fox.transpose(1, 0, 3, 2, 4).reshape(B, S, FOX_W)

    gate_a, gate_b = jnp.split(jax.nn.sigmoid(gm), 2, axis=-1)
    merged = gate_a * (o_nsa @ w_o_nsa) + gate_b * (o_fox @ w_o_fox)
    x = x + merged @ w_out

    x = x + 0.5 * _swiglu(_rms(x, ffn2_norm), ffn2_w_up, ffn2_w_down)
    return x


def setup_inputs(seed: int = 0) -> dict:
    key = jax.random.key(seed)
    ks = jax.random.split(key, 24)
    f32 = jnp.float32
    L = DEPTH

    def nrm(k, shape, fan_in):
        return jax.random.normal(k, shape, f32) * (fan_in ** -0.5)

    def gain(k, shape):
        return 1.0 + 0.1 * jax.random.normal(k, shape, f32)

    return {
        'x': jax.random.normal(ks[0], (BATCH, SEQ, D_MODEL), f32),
        'ffn1_norm': gain(ks[1], (L, D_MODEL)),
        'ffn1_w_up': nrm(ks[2], (L, D_MODEL, 2 * D_FF), D_MODEL),
        'ffn1_w_down': nrm(ks[3], (L, D_FF, D_MODEL), D_FF),
        'mix_norm': gain(ks[4], (L, D_MODEL)),
        'w_in': nrm(ks[5], (L, D_MODEL, D_IN), D_MODEL),
        'b_forget': jax.random.uniform(ks[6], (L, FOX_HEADS), f32, 1.0, 5.0),
        'nsa_q_gain': gain(ks[7], (L, HEAD_DIM)),
        'nsa_k_gain': gain(ks[8], (L, 3, HEAD_DIM)),
        'fox_q_gain': gain(ks[9], (L, HEAD_DIM)),
        'fox_k_gain': gain(ks[10], (L, HEAD_DIM)),
        'cmp_pos_k': 0.1 * jax.random.normal(ks[11], (L, CMP_BLOCK, HEAD_DIM), f32),
        'cmp_pos_v': 0.1 * jax.random.normal(ks[12], (L, CMP_BLOCK, HEAD_DIM), f32),
        'cmp_k_w1': nrm(ks[13], (L, CMP_BLOCK * HEAD_DIM, CMP_HIDDEN), CMP_BLOCK * HEAD_DIM),
        'cmp_k_w2': nrm(ks[14], (L, CMP_HIDDEN, HEAD_DIM), CMP_HIDDEN),
        'cmp_v_w1': nrm(ks[15], (L, CMP_BLOCK * HEAD_DIM, CMP_HIDDEN), CMP_BLOCK * HEAD_DIM),
        'cmp_v_w2': nrm(ks[16], (L, CMP_HIDDEN, HEAD_DIM), CMP_HIDDEN),
        'w_o_nsa': nrm(ks[17], (L, NSA_W, D_MODEL), NSA_W),
        'w_o_fox': nrm(ks[18], (L, FOX_W, D_MODEL), FOX_W),
        'w_out': nrm(ks[19], (L, D_MODEL, D_MODEL), D_MODEL),
        'ffn2_norm': gain(ks[20], (L, D_MODEL)),
        'ffn2_w_up': nrm(ks[21], (L, D_MODEL, 2 * D_FF), D_MODEL),
        'ffn2_w_down': nrm(ks[22], (L, D_FF, D_MODEL), D_FF),
        'rel_bias_table': 0.5 * jax.random.normal(ks[23], (N_BUCKETS, NSA_HEADS), f32),
    }


def reference(x, ffn1_norm, ffn1_w_up, ffn1_w_down, mix_norm, w_in, b_forget, nsa_q_gain, nsa_k_gain,
              fox_q_gain, fox_k_gain, cmp_pos_k, cmp_pos_v, cmp_k_w1, cmp_k_w2, cmp_v_w1, cmp_v_w2,
              w_o_nsa, w_o_fox, w_out, ffn2_norm, ffn2_w_up, ffn2_w_down, rel_bias_table):
    for layer in range(DEPTH):
        x = _hybrid_layer(x, ffn1_norm[layer], ffn1_w_up[layer], ffn1_w_down[layer], mix_norm[layer],
                          w_in[layer], b_forget[layer], nsa_q_gain[layer], nsa_k_gain[layer],
                          fox_q_gain[layer], fox_k_gain[layer], cmp_pos_k[layer], cmp_pos_v[layer],
                          cmp_k_w1[layer], cmp_k_w2[layer], cmp_v_w1[layer], cmp_v_w2[layer],
                          w_o_nsa[layer], w_o_fox[layer], w_out[layer], ffn2_norm[layer],
                          ffn2_w_up[layer], ffn2_w_down[layer], rel_bias_table)
    return x
```

```python
import math
from contextlib import ExitStack
import numpy as np
import ml_dtypes
import concourse.bass as bass
import concourse.mybir as mybir
from concourse.bass_utils import run_bass_kernel_spmd

F32 = mybir.dt.float32
BF16 = mybir.dt.bfloat16
AF = mybir.ActivationFunctionType
ALU = mybir.AluOpType
AX = mybir.AxisListType

ENGS = ["pe", "act", "dve", "pool", "sp"]
NDSEM = 20
S = 2048
D = 1024
DFF = 2816
NCH = 22
NEG = -30000.0
DIN = 4896


class Prog:
    def __init__(self, nc):
        self.nc = nc
        self.ops = []

    def add(self, eng, fn, r=(), w=(), dma=False):
        self.ops.append({"eng": eng, "fn": fn, "r": list(r), "w": list(w), "dma": dma})

    def barrier(self):
        for e in ENGS:
            self.ops.append({"eng": e, "fn": None, "r": [], "w": [], "dma": False, "bar": True})

    def mm(self, out, lhsT, rhs, start=True, stop=True, r=(), w=()):
        self.add("pe", lambda e: e.matmul(out, lhsT, rhs, start=start, stop=stop), r, w)

    def tr(self, out, in_, ident, r=(), w=()):
        self.add("pe", lambda e: e.transpose(out, in_, ident), r, w)

    def act(self, out, in_, func, r=(), w=(), **kw):
        self.add("act", lambda e: e.activation(out, in_, func, **kw), r, w)

    def dve(self, fn, r=(), w=()):
        self.add("dve", fn, r, w)

    def pool(self, fn, r=(), w=()):
        self.add("pool", fn, r, w)

    def dma(self, out, in_, r=(), w=(), q="sp"):
        self.add(q, lambda e: e.dma_start(out=out, in_=in_), r, w, dma=True)

    def finalize(self, stack):
        nc = self.nc
        ops = self.ops
        n = len(ops)
        last_w, readers, eng_last, last_dma, dma_n, dma_cnt = {}, {}, {}, {}, {}, {}
        deps = [None] * n
        for i, op in enumerate(ops):
            d = set()
            if op.get("bar"):
                d.update(eng_last.values())
                d.update(last_dma.values())
            else:
                for k in op["r"]:
                    if k in last_w:
                        d.add(last_w[k])
                for k in op["w"]:
                    if k in last_w:
                        d.add(last_w[k])
                    d.update(readers.get(k, ()))
                for k in op["w"]:
                    last_w[k] = i
                    readers[k] = []
                for k in op["r"]:
                    readers.setdefault(k, []).append(i)
            if op["dma"]:
                q = op["eng"]
                m = dma_n.get(q, 0)
                dma_n[q] = m + 1
                key = ("d", q, m % NDSEM)
                dma_cnt[key] = dma_cnt.get(key, 0) + 1
                op["sem"] = key
                op["val"] = 16 * dma_cnt[key]
                if key in last_dma:
                    d.add(last_dma[key])
                last_dma[key] = i
            elif not op.get("bar"):
                eng_last[op["eng"]] = i
            d.discard(i)
            red = {}
            for j in d:
                pj = ops[j]
                key = pj["sem"] if pj["dma"] else ("e", pj["eng"])
                if red.get(key, -1) < j:
                    red[key] = j
            deps[i] = set(red.values())
        signal = [False] * n
        for i in range(n):
            for j in deps[i]:
                signal[j] = True
        cnt = {e: 0 for e in ENGS}
        for i, op in enumerate(ops):
            if not op["dma"] and signal[i]:
                cnt[op["eng"]] += 1
                op["sem"] = ("e", op["eng"])
                op["val"] = cnt[op["eng"]]
        semh = {}
        for e in ENGS:
            semh[("e", e)] = stack.enter_context(nc.semaphore("s_" + e))
        for q in dma_n:
            for s in range(NDSEM):
                semh[("d", q, s)] = stack.enter_context(nc.semaphore("d_%s_%d" % (q, s)))
        per_eng = {e: [] for e in ENGS}
        for i, op in enumerate(ops):
            per_eng[op["eng"]].append(i)

        def emit(e, eng):
            known = {}
            for i in per_eng[e]:
                op = ops[i]
                need = {}
                for j in deps[i]:
                    pj = ops[j]
                    if (not pj["dma"]) and pj["eng"] == e and e == "pe":
                        continue
                    key = pj["sem"]
                    if need.get(key, 0) < pj["val"]:
                        need[key] = pj["val"]
                for key, val in need.items():
                    if known.get(key, 0) < val:
                        eng.wait_ge(semh[key], val)
                        known[key] = val
                if op["fn"] is not None:
                    inst = op["fn"](eng)
                    if op["dma"]:
                        inst.then_inc(semh[op["sem"]], 16)
                    elif signal[i]:
                        inst.then_inc(semh[op["sem"]], 1)

        with nc.Block() as block:
            @block.tensor
            def _(eng):
                emit("pe", eng)

            @block.scalar
            def _(eng):
                emit("act", eng)

            @block.vector
            def _(eng):
                emit("dve", eng)

            @block.gpsimd
            def _(eng):
                emit("pool", eng)

            @block.sync
            def _(eng):
                emit("sp", eng)


class Alloc:
    def __init__(self, big, nwords):
        self.big = big
        self.top = 0
        self.n = nwords

    def f32(self, cols, parts=128):
        off = self.top
        self.top += cols
        assert self.top <= self.n, ("sbuf overflow", self.top)
        return self.big[0:parts, off:off + cols]

    def bf16(self, cols, parts=128):
        assert cols % 2 == 0
        w = cols // 2
        off = self.top
        self.top += w
        assert self.top <= self.n, ("sbuf overflow", self.top)
        return self.big[0:parts, off:off + w].bitcast(BF16)


CB_ID, CB_BO, CB_E, CB_MASK, CB_OV, CB_SELC, CB_ONES, NCB = 0, 128, 256, 2304, 4352, 4384, 6432, 6560
CF_ID, CF_BONUS, CF_OH, CF_FAR, CF_ONES33, NCF = 0, 128, 640, 1024, 1152, 1280
PK_G1, PK_GM, PK_G2, PK_GQN, PK_GKC, PK_GKS, PK_GKW, PK_GQF, PK_GKF, PK_BF, PK_TAB, PK_POSK, PK_POSV, NPK = \
    0, 8, 16, 24, 25, 26, 27, 28, 29, 30, 32, 40, 72, 104


def _t5_bucket_np(n):
    n = np.maximum(n, 0)
    nf = np.maximum(n, 1).astype(np.float32)
    large = 16 + (np.log(nf / np.float32(16)) / np.float32(math.log(128 / 16)) * np.float32(16)).astype(np.int32)
    large = np.minimum(large, 31)
    return np.where(n < 16, n, large)


def _host_consts():
    cb = np.zeros((128, NCB), np.float32)
    cb[:, CB_ID:CB_ID + 128] = np.eye(128)
    p = np.arange(128)
    cb[:, CB_BO:CB_BO + 128] = (p[:, None] // 64 == p[None, :] // 64)
    for j in range(16):
        for kl in range(128):
            cb[2 * j + kl // 64, CB_E + j * 128 + kl] = 1.0
    for m in range(4):
        qc = np.arange(512)
        cb[:, CB_MASK + m * 512:CB_MASK + (m + 1) * 512] = np.where(qc[None, :] >= 128 * m + p[:, None], 0.0, NEG)
    ci = np.arange(127)[:, None] * 16
    sj = np.arange(32)[None, :] * 64
    cb[0:127, CB_OV:CB_OV + 32] = ((ci <= sj + 63) & (ci + 31 >= sj))
    for i in range(16):
        for c in range(127):
            cp = c - 8 * i
            if -9 <= cp <= 6:
                cb[6 - cp, CB_SELC + i * 128 + c] = 1.0
            elif cp < -9:
                cb[16, CB_SELC + i * 128 + c] = 1.0
            else:
                cb[17, CB_SELC + i * 128 + c] = 1.0
    cb[:, CB_ONES:CB_ONES + 128] = 1.0
    cf = np.zeros((128, NCF), np.float32)
    cf[:, CF_ID:CF_ID + 128] = np.eye(128)
    for i in range(16):
        t = 128 * i + p
        cur = t // 64
        blk = np.arange(32)
        forced = (blk[None, :] == 0) | (blk[None, :] == cur[:, None]) | (blk[None, :] == cur[:, None] - 1)
        cf[:, CF_BONUS + i * 32:CF_BONUS + (i + 1) * 32] = np.where(blk[None, :] <= cur[:, None], 1.0e4 * forced, -1.0e30)
    m = np.arange(384)
    dd = m - 127
    bk = _t5_bucket_np(dd)
    for b in range(32):
        cf[b, CF_OH:CF_OH + 384] = ((dd >= 0) & (bk == b))
    cf[32, CF_OH:CF_OH + 384] = (dd < 0)
    cf[:, CF_FAR:CF_FAR + 128] = (p[None, :] < p[:, None])
    cf[0:33, CF_ONES33:CF_ONES33 + 128] = 1.0
    return cb.astype(ml_dtypes.bfloat16), cf


def _pack_params(inp):
    pk = np.zeros((128, NPK), np.float32)
    pk[:, PK_G1:PK_G1 + 8] = inp["ffn1_norm"][0].reshape(8, 128).T
    pk[:, PK_GM:PK_GM + 8] = inp["mix_norm"][0].reshape(8, 128).T
    pk[:, PK_G2:PK_G2 + 8] = inp["ffn2_norm"][0].reshape(8, 128).T
    pk[:, PK_GQN] = np.tile(inp["nsa_q_gain"][0], 2)
    pk[:, PK_GKC] = np.tile(inp["nsa_k_gain"][0, 0], 2)
    pk[:, PK_GKS] = np.tile(inp["nsa_k_gain"][0, 1], 2)
    pk[:, PK_GKW] = np.tile(inp["nsa_k_gain"][0, 2], 2)
    pk[:, PK_GQF] = np.tile(inp["fox_q_gain"][0], 2)
    pk[:, PK_GKF] = np.tile(inp["fox_k_gain"][0], 2)
    pk[0:8, PK_BF] = inp["b_forget"][0]
    pk[0:32, PK_TAB:PK_TAB + 8] = inp["rel_bias_table"]
    pk[32, PK_TAB:PK_TAB + 8] = NEG
    pk[:, PK_POSK:PK_POSK + 32] = np.tile(inp["cmp_pos_k"][0].T, (2, 1))
    pk[:, PK_POSV:PK_POSV + 32] = np.tile(inp["cmp_pos_v"][0].T, (2, 1))
    return pk


class _Stop(Exception):
    pass


def build_nc(nseq=2, stop=None):
    nc = bass.Bass("TRN2", target_bir_lowering=False)
    dbg_n = [0]

    def dbgout(P, name, src):
        o = nc.dram_tensor("dbg_" + name, list(src.shape), src.dtype, kind="ExternalOutput").ap()
        P.dma(o, src)

    def check(P, tag, outs):
        if stop == tag:
            P.barrier()
            for nm, src in outs:
                dbgout(P, nm, src)
            raise _Stop()

    def din(name, shape, dt=F32):
        return nc.dram_tensor(name, list(shape), dt, kind="ExternalInput").ap()

    def dscr(name, shape, dt=BF16):
        return nc.dram_tensor(name, list(shape), dt, kind="Internal").ap()

    x_d = din("x", [nseq, S, D])
    y_d = nc.dram_tensor("y", [nseq, S, D], F32, kind="ExternalOutput").ap()
    w_up_d = [din("ffn1_w_up", [D, 2 * DFF]), din("ffn2_w_up", [D, 2 * DFF])]
    w_dn_d = [din("ffn1_w_down", [DFF, D]), din("ffn2_w_down", [DFF, D])]
    w_in_d = din("w_in", [D, DIN])
    cw1_d = [din("cmp_k_w1", [2048, 128]), din("cmp_v_w1", [2048, 128])]
    cw2_d = [din("cmp_k_w2", [128, 64]), din("cmp_v_w2", [128, 64])]
    won_d = din("w_o_nsa", [512, D])
    wof_d = din("w_o_fox", [512, D])
    wout_d = din("w_out", [D, D])
    cb_d = din("cb", [128, NCB], BF16)
    cf_d = din("cf", [128, NCF])
    pk_d = din("pk", [128, NPK])

    wup_s = [dscr("wup1s", [D, 2 * DFF]), dscr("wup2s", [D, 2 * DFF])]
    wdn_s = [dscr("wdn1s", [DFF, D]), dscr("wdn2s", [DFF, D])]
    winA_s = dscr("winAs", [D, 1304])
    winB_s = dscr("winBs", [D, 1544])
    winG_s = dscr("winGs", [D, 2048])
    cw1_s = [dscr("cw1ks", [2048, 128]), dscr("cw1vs", [2048, 128])]
    cw2_s = [dscr("cw2ks", [128, 64]), dscr("cw2vs", [128, 64])]
    won_s = dscr("wons", [512, D])
    wof_s = dscr("wofs", [512, D])
    wout_s = dscr("wouts", [D, D])
    x1_s = dscr("x1s", [nseq, S, D], F32)
    buf_s = dscr("bufs", [8, 128, 384], F32)
    aug_s = dscr("augs", [2, 6, 8, S], BF16)

    st = ExitStack()
    with st:
        NW = 52800
        big = st.enter_context(nc.sbuf_tensor("big", [128, NW], F32))
        ps = [st.enter_context(nc.psum_tensor("ps%d" % i, [128, 512], F32)) for i in range(8)]
        P = Prog(nc)
        A = Alloc(big, NW)
        def body():

            def psb(i):
                return ps[i][:, :].bitcast(BF16)

            cb = A.bf16(NCB)
            cf = A.f32(NCF)
            pk = A.f32(NPK)
            sm = A.f32(8)
            Tb = {}
            for nm in ("T0", "T128", "Tfar"):
                for g in range(2):
                    Tb[nm, g] = A.bf16(512)
            Gext = [A.bf16(512), A.bf16(512)]
            rc = A.f32(4)
            w4 = A.f32(4)
            P.dma(cb, cb_d, w=["cb"])
            P.dma(cf, cf_d, w=["cf"])
            P.dma(pk, pk_d, w=["pk"])
            P.pool(lambda e: e.memset(sm[:, 0:1], 1e-6), w=["sm"])
            P.pool(lambda e: e.memset(sm[:, 1:2], 1.0), w=["sm"])
            P.dve(lambda e: e.tensor_scalar(sm[:, 2:3], pk[:, PK_GQN:PK_GQN + 1], 0.125, None, ALU.mult), r=["pk", "sm"], w=["sm"])
            P.dve(lambda e: e.tensor_scalar(sm[:, 3:4], pk[:, PK_GQF:PK_GQF + 1], 0.125, None, ALU.mult), r=["pk", "sm"], w=["sm"])
            P.dve(lambda e: e.tensor_scalar(sm[:, 4:5], pk[:, PK_BF:PK_BF + 1], -1.0, None, ALU.mult), r=["pk", "sm"], w=["sm"])
            ident = cb[:, CB_ID:CB_ID + 128]
            blockones = cb[:, CB_BO:CB_BO + 128]
            identf = cf[:, CF_ID:CF_ID + 128]
            epsb = sm[:, 0:1]
            oneb = sm[:, 1:2]
            base_top = A.top

            wpending = []
            wstage = {}
            wcnt = [0]

            def convert(src, R, c0, c1, segs):
                assert c1 - c0 <= 2048
                for rc in range(R // 128):
                    def piece(rc=rc):
                        st32, st16 = wstage["st32"], wstage["st16"]
                        s = wcnt[0] % 2
                        wd = c1 - c0
                        k32, k16 = ("stg0", "stg1")[s], ("stg2a", "stg2b")[s]
                        P.dma(st32[s][:, 0:wd], src[rc * 128:(rc + 1) * 128, c0:c1], w=[k32])
                        o_, i_ = st16[s][:, 0:wd], st32[s][:, 0:wd]
                        if wcfg["late"]:
                            P.pool(lambda e: e.tensor_copy(o_, i_), r=[k32], w=[k16])
                        elif wcnt[0] % 2 == 0:
                            P.dve(lambda e: e.tensor_copy(o_, i_), r=[k32, "stg2"], w=[k16])
                        else:
                            P.act(o_, i_, AF.Copy, r=[k32, "stg2"], w=[k16])
                        for (sc, wdt, dst, dc) in segs:
                            P.dma(dst[rc * 128:(rc + 1) * 128, dc:dc + wdt], st16[s][:, sc - c0:sc - c0 + wdt], r=[k16], q="pool")
                        wcnt[0] += 1
                    wpending.append(piece)

            segsA = []
            for g in range(2):
                for r in range(4):
                    segsA.append(((4 * g + r) * 64, 64, winA_s, r * 128 + g * 64))
            segsA += [(512, 128, winA_s, 512), (640, 128, winA_s, 640), (768, 128, winA_s, 768),
                      (1024, 128, winA_s, 896), (896, 128, winA_s, 1024), (1152, 128, winA_s, 1152),
                      (1280, 24, winA_s, 1280)]
            convert(w_in_d, D, 0, 1304, segsA)
            for kv in range(2):
                convert(cw1_d[kv], 2048, 0, 128, [(0, 128, cw1_s[kv], 0)])
                convert(cw2_d[kv], 128, 0, 64, [(0, 64, cw2_s[kv], 0)])
            n_early = len(wpending)
            convert(w_in_d, D, 1304, 2848, [(1304, 1544, winB_s, 0)])
            convert(w_in_d, D, 2848, 4896, [(2848, 2048, winG_s, 0)])
            convert(won_d, 512, 0, 1024, [(0, 1024, won_s, 0)])
            convert(wof_d, 512, 0, 1024, [(0, 1024, wof_s, 0)])
            convert(wout_d, D, 0, 1024, [(0, 1024, wout_s, 0)])
            wlate = wpending[n_early:]
            del wpending[n_early:]
            wcfg = {"late": False}

            Xt = A.f32(8 * 384, parts=33)
            rep = A.f32(8 * 384)
            for h in range(8):
                P.dve(lambda e, h=h: e.tensor_scalar(Xt[:, h * 384:(h + 1) * 384], cf[0:33, CF_OH:CF_OH + 384],
                                                     pk[0:33, PK_TAB + h:PK_TAB + h + 1], None, ALU.mult),
                      r=["cf", "pk"], w=["Xt"])
            for q in range(6):
                P.mm(ps[q][:, :], cf[0:33, CF_ONES33:CF_ONES33 + 128], Xt[:, q * 512:(q + 1) * 512], r=["cf", "Xt"], w=["ps%d" % q])
                P.act(rep[:, q * 512:(q + 1) * 512], ps[q][:, :], AF.Copy, r=["ps%d" % q], w=["rep"])
            b31c = A.f32(8)
            for h in range(8):
                P.dve(lambda e, h=h: e.tensor_copy(b31c[:, h:h + 1], rep[:, h * 384 + 327:h * 384 + 328]), r=["rep", "b31c"], w=["b31c"])
            for h in range(8):
                P.dve(lambda e, h=h: e.tensor_scalar(rep[:, h * 384:(h + 1) * 384], rep[:, h * 384:(h + 1) * 384], b31c[:, h:h + 1], None,
                                                     ALU.subtract), r=["rep", "b31c"], w=["rep"])
            P.dma(buf_s.rearrange("h k m -> k h m"), rep.rearrange("p (h m) -> p h m", h=8), r=["rep"], w=["bufs"])
            tst = A.f32(512)
            for g in range(2):
                for nm, delta in (("T0", 0), ("T128", 128)):
                    src = bass.AP(buf_s.tensor, delta + 127 + 4 * g * 49152, [[383, 128], [49152, 4], [1, 128]])
                    P.dma(tst.rearrange("p (r q) -> p r q", r=4), src, r=["bufs"], w=["tst"])
                    P.dve(lambda e, nm=nm, g=g: e.tensor_copy(Tb[nm, g], tst), r=["tst"], w=["Tb"])
                tst2 = A.f32(128)
                P.dve(lambda e, tst2=tst2: e.tensor_scalar(tst2, cf[:, CF_FAR:CF_FAR + 128], -NEG, NEG, ALU.mult, ALU.add),
                      r=["cf", "tst2"], w=["tst2"])
                P.dve(lambda e, g=g, tst2=tst2: e.tensor_copy(Tb["Tfar", g].rearrange("p (r q) -> p r q", r=4),
                                                              tst2.unsqueeze(1).to_broadcast([128, 4, 128])),
                      r=["tst2"], w=["Tb"])
                zf = A.f32(512)
                negf = A.f32(512)
                P.pool(lambda e, zf=zf: e.memset(zf, 0.0), w=["zf"])
                P.pool(lambda e, negf=negf: e.memset(negf, NEG), w=["negf"])
                gst = A.f32(512, parts=18)
                srcg = bass.AP(buf_s.tensor, 4 * g * 49152, [[400, 16], [49152, 4], [1, 128]])
                P.dma(gst[0:16, :].rearrange("p (r q) -> p r q", r=4), srcg, r=["bufs"], w=["gst"])
                P.dma(gst[16:17, :], zf[0:1, :], r=["zf", "gst"], w=["gst"])
                P.dma(gst[17:18, :], negf[0:1, :], r=["negf", "gst"], w=["gst"])
                P.dve(lambda e, g=g, gst=gst: e.tensor_copy(Gext[g][0:18, :], gst), r=["gst"], w=["Gext"])
            P.barrier()
            check(P, "W", [("T0", Tb["T0", 0]), ("T128", Tb["T128", 1]), ("Tfar", Tb["Tfar", 0]),
                           ("Gext", Gext[0][0:18, :]), ("winA", winA_s[0:128, :]), ("wout", wout_s[896:1024, :])])
            A.top = base_top

            cnt = {"xs": 0, "S": 0, "P": 0, "O": 0, "w": 0, "n": 0}

            def rms_T(xt, xkey, gcol, dst_fn, dkey, W, defer=False):
                s = cnt["xs"] % 2
                cnt["xs"] += 1
                junk, ss, rstd, xs = W["junk"], W["ss"][s], W["rstd"][s], W["xs"][s]
                P.act(junk, xt, AF.Square, r=[xkey], w=["junk", "ss%d" % s], accum_out=ss)
                P.act(rstd, ss, AF.Ln, r=["ss%d" % s, "sm"], w=["rstd%d" % s], scale=1.0 / D, bias=epsb)
                P.act(rstd, rstd, AF.Exp, r=["rstd%d" % s], w=["rstd%d" % s], scale=-0.5)
                P.dve(lambda e: e.tensor_scalar(xs, xt, rstd, None, ALU.mult), r=[xkey, "rstd%d" % s], w=["xs%d" % s])

                def part_b():
                    pb = psb(6).rearrange("p (k t) -> p k t", k=8)
                    for k in range(8):
                        P.tr(pb[:, k, :], xs[:, k * 128:(k + 1) * 128], ident, r=["xs%d" % s, "cb"], w=["ps6"])
                    gT = pk[:, gcol:gcol + 8].unsqueeze(2).to_broadcast([128, 8, 128])
                    P.dve(lambda e: e.tensor_tensor(dst_fn(), pb, gT, ALU.mult), r=["ps6", "pk"], w=[dkey])
                if defer:
                    return part_b
                part_b()
                return None

            def ffn(f, b, load_x, store_out, make_u, uT):
                top0 = A.top
                W = {"junk": A.bf16(1024), "ss": [A.f32(1), A.f32(1)], "rstd": [A.f32(1), A.f32(1)],
                     "xs": [A.bf16(1024), A.bf16(1024)]}
                wd = A.bf16(NCH * 1024).rearrange("p (c n) -> p c n", c=NCH)
                wg = [A.bf16(8 * 256).rearrange("p (k n) -> p k n", k=8) for _ in range(2)]
                wu = [A.bf16(8 * 256).rearrange("p (k n) -> p k n", k=8) for _ in range(2)]
                hT = A.bf16(NCH * 512).rearrange("p (c n) -> p c n", c=NCH)
                xnT = A.bf16(8 * 512).rearrange("p (k n) -> p k n", k=8)
                xin = [A.f32(1024) for _ in range(4)]
                sg = [A.f32(512), A.f32(512)]
                gcol = PK_G1 if f == 0 else PK_G2
                carry = [None]
                first = (b == 0)
                wdv = wdn_s[f].rearrange("(c p) n -> p c n", p=128)
                wupv = wup_s[f].rearrange("(k p) n -> p k n", p=128)
                if first:
                    stg = [A.f32(2048), A.f32(2048), A.f32(2048)]
                    wup32 = w_up_d[f].rearrange("(k p) n -> p k n", p=128)
                    wdn32 = w_dn_d[f].rearrange("(c p) n -> p c n", p=128)
                else:
                    P.dma(wd, wdv, r=["wdnS%d_%d" % (f, q) for q in range(11)], w=["wd"])
                for tb in range(4):
                    for t in range(4):
                        tt = tb * 4 + t
                        load_x(xin[t], "xin%d" % t, tt)
                        rms_T(xin[t], "xin%d" % t, gcol, lambda t=t: xnT[:, :, t * 128:(t + 1) * 128], "xnT", W)
                        if t == 0 and carry[0] is not None:
                            carry[0]()
                            carry[0] = None
                    for cp in range(11):
                        s = cnt["w"] % 2
                        cnt["w"] += 1
                        gk, uk = "wupS%d_g%d" % (f, cp), "wupS%d_u%d" % (f, cp)
                        if first and tb == 0:
                            P.dma(stg[0].rearrange("p (k n) -> p k n", k=8), wup32[:, :, cp * 256:(cp + 1) * 256], w=["stg0"])
                            P.pool(lambda e, s=s: e.tensor_copy(wg[s], stg[0].rearrange("p (k n) -> p k n", k=8)), r=["stg0"], w=["wg%d" % s])
                            P.dma(wupv[:, :, cp * 256:(cp + 1) * 256], wg[s], r=["wg%d" % s], w=[gk], q="pool")
                            P.dma(stg[1].rearrange("p (k n) -> p k n", k=8), wup32[:, :, DFF + cp * 256:DFF + (cp + 1) * 256], w=["stg1"])
                            P.dve(lambda e, s=s: e.tensor_copy(wu[s], stg[1].rearrange("p (k n) -> p k n", k=8)), r=["stg1"], w=["wu%d" % s])
                            P.dma(wupv[:, :, DFF + cp * 256:DFF + (cp + 1) * 256], wu[s], r=["wu%d" % s], w=[uk], q="pool")
                            P.dma(stg[2].rearrange("p (c n) -> p c n", c=2), wdn32[:, 2 * cp:2 * cp + 2, :], w=["stg2"])
                            P.act(wd[:, 2 * cp:2 * cp + 2, :], stg[2].rearrange("p (c n) -> p c n", c=2), AF.Copy, r=["stg2", "wd"], w=["wd"])
                            P.dma(wdv[:, 2 * cp:2 * cp + 2, :], wd[:, 2 * cp:2 * cp + 2, :], r=["wd"], w=["wdnS%d_%d" % (f, cp)], q="pool")
                        else:
                            P.dma(wg[s], wupv[:, :, cp * 256:(cp + 1) * 256], r=[gk], w=["wg%d" % s])
                            P.dma(wu[s], wupv[:, :, DFF + cp * 256:DFF + (cp + 1) * 256], r=[uk], w=["wu%d" % s])
                            if first and f == 0:
                                wstage["st32"] = [stg[0], stg[1]]
                                wstage["st16"] = [stg[2].bitcast(BF16)[:, 0:2048], stg[2].bitcast(BF16)[:, 2048:4096]]
                                for _ in range(3):
                                    if wpending:
                                        wpending.pop(0)()
                        for cc in range(2):
                            c = cp * 2 + cc
                            pg, pu = c % 2, 2 + c % 2
                            for k in range(8):
                                P.mm(ps[pg][:, :], wg[s][:, k, cc * 128:(cc + 1) * 128], xnT[:, k, :], start=(k == 0), stop=(k == 7),
                                     r=["wg%d" % s, "xnT"], w=["ps%d" % pg])
                            for k in range(8):
                                P.mm(ps[pu][:, :], wu[s][:, k, cc * 128:(cc + 1) * 128], xnT[:, k, :], start=(k == 0), stop=(k == 7),
                                     r=["wu%d" % s, "xnT"], w=["ps%d" % pu])
                            P.act(sg[c % 2], ps[pg][:, :], AF.Silu, r=["ps%d" % pg], w=["sg%d" % (c % 2)])
                            P.dve(lambda e, c=c, pu=pu: e.tensor_tensor(hT[:, c, :], ps[pu][:, :], sg[c % 2], ALU.mult),
                                  r=["ps%d" % pu, "sg%d" % (c % 2)], w=["hT%d" % c])
                    for t in range(4):
                        tt = tb * 4 + t
                        for dh in range(2):
                            pd = 4 + dh
                            for c in range(NCH):
                                P.mm(ps[pd][:, :], hT[:, c, t * 128:(t + 1) * 128], wd[:, c, dh * 512:(dh + 1) * 512],
                                     start=(c == 0), stop=(c == NCH - 1), r=["hT%d" % c, "wd"], w=["ps%d" % pd])
                            xs_ = xin[t][:, dh * 512:(dh + 1) * 512]
                            P.dve(lambda e, pd=pd, xs_=xs_: e.scalar_tensor_tensor(xs_, ps[pd][:, :], 0.5, xs_, ALU.mult, ALU.add),
                                  r=["ps%d" % pd, "xin%d" % t], w=["xin%d" % t])
                        store_out(xin[t], "xin%d" % t, tt)
                        if make_u:
                            nb = rms_T(xin[t], "xin%d" % t, PK_GM, lambda tt=tt: uT[:, :, tt * 128:(tt + 1) * 128], "uT", W, defer=True)
                            if carry[0] is not None:
                                carry[0]()
                            carry[0] = nb
                if carry[0] is not None:
                    carry[0]()
                    carry[0] = None
                if first and f == 0:
                    while wpending:
                        wpending.pop(0)()
                P.barrier()
                A.top = top0

            def qk_norm(psrc, np_, ncol, gcol_ap, dst, skey, dkey, W):
                s = cnt["n"] % 2
                cnt["n"] += 1
                sq, rs = W["sq"][s], W["rs"][s]
                P.act(sq[0:np_, 0:ncol], psrc, AF.Square, r=[skey], w=["sq%d" % s])
                P.mm(ps[7][0:np_, 0:ncol], blockones[0:np_, 0:np_], sq[0:np_, 0:ncol], r=["sq%d" % s, "cb"], w=["ps7"])
                P.act(rs[0:np_, 0:ncol], ps[7][0:np_, 0:ncol], AF.Ln, r=["ps7", "sm"], w=["rs%d" % s], scale=1.0 / 64, bias=epsb[0:np_, :])
                P.act(rs[0:np_, 0:ncol], rs[0:np_, 0:ncol], AF.Exp, r=["rs%d" % s], w=["rs%d" % s], scale=-0.5)
                dsts = dst if isinstance(dst, list) else [(slice(0, np_), dst)]
                for (psl, d_) in dsts:
                    def v3(ap, d_=d_):
                        return ap if len(d_.shape) == 2 else ap.rearrange("p (a q) -> p a q", a=d_.shape[1])
                    P.dve(lambda e, psl=psl, d_=d_, v3=v3: e.scalar_tensor_tensor(d_, v3(psrc[psl]), gcol_ap[psl], v3(rs[psl, 0:ncol]),
                                                                             ALU.mult, ALU.mult),
                          r=[skey, "rs%d" % s, "sm", "pk"], w=[dkey])

            def run_tiles(tiles):
                def emit_S(tl):
                    sb = cnt["S"] % 3
                    cnt["S"] += 1
                    tl["sb"] = sb
                    n = len(tl["mms"])
                    for q, (l_, r_, kk) in enumerate(tl["mms"]):
                        P.mm(ps[sb][0:tl["kp"], :], l_, r_, start=(q == 0), stop=(q == n - 1), r=kk, w=["ps%d" % sb])

                def emit_E(tl):
                    pslot = cnt["P"] % 3
                    cnt["P"] += 1
                    tl["pt"] = pslot
                    sb = tl["sb"]
                    P.act(Pt[pslot][0:tl["kp"], :], ps[sb][0:tl["kp"], :], AF.Exp, r=["ps%d" % sb], w=["Pt%d" % pslot])

                def emit_PV(tl):
                    pslot = tl["pt"]
                    Ob = tl["Ob"]
                    okey = "ps%d" % Ob
                    oc = tl["ocols"]
                    if tl.get("init"):
                        P.mm(ps[Ob][:, 0:tl["init"] * oc], cb[0:1, CB_E + 128:CB_E + 256], cb[0:1, CB_E + 128:CB_E + 128 + tl["init"] * oc],
                             start=True, stop=False, r=["cb"], w=[okey])
                    npv = len(tl["pv"])
                    for q_, (sub, rhs, sp_, kk) in enumerate(tl["pv"]):
                        last = bool(tl.get("after")) and q_ == npv - 1
                        P.mm(ps[Ob][:, sub * oc:(sub + 1) * oc], Pt[pslot][0:tl["kp"], sub * 128:(sub + 1) * 128], rhs,
                             start=False, stop=last, r=["Pt%d" % pslot] + kk, w=[okey])
                    if tl.get("after"):
                        tl["after"]()

                emit_S(tiles[0])
                for idx, tl in enumerate(tiles):
                    if idx + 1 < len(tiles):
                        emit_S(tiles[idx + 1])
                    emit_E(tl)
                    emit_PV(tl)

            def next_O():
                ob = 3 + cnt["O"] % 3
                cnt["O"] += 1
                return ob

            for b in range(nseq):
                uT = A.bf16(8 * S).rearrange("p (k n) -> p k n", k=8)

                def load_x1(xt, key, tt, b=b):
                    P.dma(xt, x_d[b, tt * 128:(tt + 1) * 128, :], w=[key])

                def store_x1(xt, key, tt, b=b):
                    P.dma(x1_s[b, tt * 128:(tt + 1) * 128, :], xt, r=[key], w=["x1s_%d" % tt], q="pool")

                ffn(0, b, load_x1, store_x1, True, uT)
                check(P, "P1", [("x1", x1_s[0]), ("uT", uT.rearrange("p k n -> p (k n)"))])
                oT = A.bf16(8 * S).rearrange("p (k n) -> p k n", k=8)
                seq_top = A.top

                Pt = [A.bf16(512) for _ in range(3)]
                Wn = {"sq": [A.bf16(512), A.bf16(512)], "rs": [A.f32(512), A.f32(512)]}
                att_top = A.top
                QA = [A.bf16(16 * 512).rearrange("p (i r q) -> p i r q", i=16, r=4) for _ in range(2)]
                KA = [A.bf16(S), A.bf16(S)]
                RA = [slice(0, 96), None]
                RS = [slice(64, 96), slice(0, 32)]
                KTw = A.bf16(S)
                rawk = A.bf16(S)
                rawv = A.bf16(S)
                Vs = A.bf16(16 * 2 * 66).rearrange("p (t g d) -> p t g d", t=16, g=2)
                Vw = A.bf16(16 * 2 * 66).rearrange("p (t g d) -> p t g d", t=16, g=2)
                gn = A.f32(16 * 24).rearrange("p (t c) -> p t c", t=16)
                KTc = A.bf16(128)
                Vc = A.bf16(2 * 98).rearrange("p (g d) -> p g d", g=2)
                top_a = A.top
                wA = A.bf16(8 * 1304).rearrange("p (k n) -> p k n", k=8)
                P.dma(wA, winA_s.rearrange("(k p) n -> p k n", p=128), w=["wA"])
                P.dma(KA[0][64:96, :], cb[0:32, CB_E:CB_E + 2048], r=["cb"], w=["KAe0"])
                P.dma(KA[1][0:32, :], cb[0:32, CB_E:CB_E + 2048], r=["cb"], w=["KAe1"])
                P.pool(lambda e: e.memset(KA[1][32:64, :], 0.0), w=["KAz1"])
                P.pool(lambda e: e.memset(QA[1][32:64, :, :, :].rearrange("p i r q -> p (i r q)"), 0.0), w=["QAz1"])
                P.pool(lambda e: e.memset(Vs.rearrange("p t g d -> p (t g d)"), 1.0), w=["Vs"])
                P.pool(lambda e: e.memset(Vw.rearrange("p t g d -> p (t g d)"), 1.0), w=["Vw"])
                for tb in range(4):
                    ub = lambda k, tb=tb: uT[:, k, tb * 512:(tb + 1) * 512]
                    chunks = [("q", r, r * 128) for r in range(4)] + [("ks", 0, 768), ("kw", 0, 896), ("kc", 0, 512), ("vc", 0, 640)]
                    for ci, (kind, r, col) in enumerate(chunks):
                        pb_ = ci % 2
                        for k in range(8):
                            P.mm(ps[pb_][:, :], wA[:, k, col:col + 128], ub(k), start=(k == 0), stop=(k == 7),
                                 r=["wA", "uT"], w=["ps%d" % pb_])
                        if kind == "q":
                            qk_norm(ps[pb_][:, :], 128, 512, sm[:, 2:3],
                                    [(slice(0, 64), QA[0][0:64, 4 * tb:4 * tb + 4, r, :]), (slice(64, 128), QA[1][64:128, 4 * tb:4 * tb + 4, r, :])],
                                    "ps%d" % pb_, "QTn", Wn)
                        elif kind == "ks":
                            qk_norm(ps[pb_][:, :], 128, 512, pk[:, PK_GKS:PK_GKS + 1],
                                    [(slice(0, 64), KA[0][0:64, tb * 512:(tb + 1) * 512]), (slice(64, 128), KA[1][64:128, tb * 512:(tb + 1) * 512])],
                                    "ps%d" % pb_, "KTs", Wn)
                        elif kind == "kw":
                            qk_norm(ps[pb_][:, :], 128, 512, pk[:, PK_GKW:PK_GKW + 1], KTw[:, tb * 512:(tb + 1) * 512], "ps%d" % pb_, "KTw", Wn)
                        else:
                            dst = (rawk if kind == "kc" else rawv)[:, tb * 512:(tb + 1) * 512]
                            P.act(dst, ps[pb_][:, :], AF.Copy, r=["ps%d" % pb_], w=["raw" + kind])
                for tt in range(16):
                    pb_ = 2 + tt % 2
                    for k in range(8):
                        P.mm(ps[pb_][:, 0:280], uT[:, k, tt * 128:(tt + 1) * 128], wA[:, k, 1024:1304], start=(k == 0), stop=(k == 7),
                             r=["wA", "uT"], w=["ps%d" % pb_])
                    if tt == 0:
                        check(P, "P2a2", [("gn", gn.rearrange("p t c -> p (t c)"))])
                    P.act(Vs[:, tt, :, 0:64], ps[pb_][:, 0:128].rearrange("p (g d) -> p g d", g=2), AF.Copy, r=["ps%d" % pb_, "Vs"], w=["Vs"])
                    if tt == 0:
                        check(P, "P2a3", [("gn", gn.rearrange("p t c -> p (t c)"))])
                    P.act(Vw[:, tt, :, 0:64], ps[pb_][:, 128:256].rearrange("p (g d) -> p g d", g=2), AF.Copy, r=["ps%d" % pb_, "Vw"], w=["Vw"])
                    if tt == 0:
                        check(P, "P2a4", [("gn", gn.rearrange("p t c -> p (t c)"))])
                    P.act(gn[:, tt, :], ps[pb_][:, 256:280], AF.Sigmoid, r=["ps%d" % pb_, "gn"], w=["gn"])
                    if tt == 0:
                        check(P, "P2a5", [("gn", gn.rearrange("p t c -> p (t c)"))])
                P.barrier()
                check(P, "P2a", [("QA0", QA[0].rearrange("p i r q -> p (i r q)")), ("QA1", QA[1].rearrange("p i r q -> p (i r q)")),
                                 ("KA0", KA[0]), ("KA1", KA[1]), ("KTw", KTw), ("rawk", rawk),
                                 ("Vs", Vs.rearrange("p t g d -> p (t g d)")), ("gn", gn.rearrange("p t c -> p (t c)"))])
                A.top = top_a

                w1 = [A.bf16(32 * 128).rearrange("p (l h) -> p l h", l=32) for _ in range(2)]
                w2k = A.bf16(192)
                w2v = A.bf16(64)
                posT = [A.bf16(32), A.bf16(32)]
                hb = [A.f32(1), A.f32(1)]
                hs = [[A.bf16(128) for _ in range(2)] for _ in range(2)]
                for kv in range(2):
                    srcw = cw1_s[kv].rearrange("(l d) h -> d l h", d=64)
                    P.dma(w1[kv][0:64], srcw, w=["w1_%d" % kv])
                    P.dma(w1[kv][64:128], srcw, r=["w1_%d" % kv], w=["w1_%d" % kv])
                    pcol = PK_POSK if kv == 0 else PK_POSV
                    P.dve(lambda e, kv=kv, pcol=pcol: e.tensor_copy(posT[kv], pk[:, pcol:pcol + 32]), r=["pk"], w=["posT%d" % kv])
                P.pool(lambda e: e.memset(w2k, 0.0), w=["w2k"])
                P.dma(w2k[:, 64:128], cw2_s[0], r=["w2k"], w=["w2k"])
                P.dma(w2v, cw2_s[1], w=["w2v"])
                for kv in range(2):
                    for l in range(32):
                        P.mm(ps[0][:, 0:1], w1[kv][0:64, l, :], posT[kv][0:64, l:l + 1], start=(l == 0), stop=(l == 31),
                             r=["w1_%d" % kv, "posT%d" % kv], w=["ps0"])
                    P.act(hb[kv], ps[0][:, 0:1], AF.Copy, r=["ps0"], w=["hb%d" % kv])
                    raw = rawk if kv == 0 else rawv
                    for g in range(2):
                        pb_ = 1 + g
                        for l in range(32):
                            P.mm(ps[pb_][:, 0:127], w1[kv][g * 64:(g + 1) * 64, l, :],
                                 raw.rearrange("p (c s) -> p c s", s=16)[g * 64:(g + 1) * 64, (l // 16):(l // 16) + 127, l % 16],
                                 start=(l == 0), stop=(l == 31), r=["w1_%d" % kv, "rawkc" if kv == 0 else "rawvc"], w=["ps%d" % pb_])
                        P.act(hs[kv][g][:, 0:127], ps[pb_][:, 0:127], AF.Silu, r=["ps%d" % pb_, "hb%d" % kv], w=["hs%d%d" % (kv, g)], bias=hb[kv])
                P.mm(ps[3][:, 0:127], w2k[:, 64:192], hs[0][0][:, 0:127], start=True, stop=False, r=["w2k", "hs00"], w=["ps3"])
                P.mm(ps[3][:, 0:127], w2k[:, 0:128], hs[0][1][:, 0:127], start=False, stop=True, r=["w2k", "hs01"], w=["ps3"])
                qk_norm(ps[3][:, 0:127], 128, 127, pk[:, PK_GKC:PK_GKC + 1], KTc[:, 0:127], "ps3", "KTc", {"sq": [Wn["sq"][0][:, 0:128], Wn["sq"][1][:, 0:128]],
                                                                                                "rs": [Wn["rs"][0][:, 0:128], Wn["rs"][1][:, 0:128]]})
                P.pool(lambda e: e.memset(Vc.rearrange("p g d -> p (g d)"), 1.0), w=["Vc"])
                for g in range(2):
                    P.mm(ps[4 + g][0:127, 0:64], hs[1][g][:, 0:127], w2v, r=["hs1%d" % g, "w2v"], w=["ps%d" % (4 + g)])
                    P.act(Vc[0:127, g, 0:64], ps[4 + g][0:127, 0:64], AF.Copy, r=["ps%d" % (4 + g), "Vc"], w=["Vc"])
                    P.dve(lambda e, g=g: e.tensor_copy(Vc[0:127, g, 65:97], cb[0:127, CB_OV:CB_OV + 32]), r=["cb", "Vc"], w=["Vc"])
                P.barrier()
                check(P, "P3", [("KTc", KTc), ("Vc", Vc.rearrange("p g d -> p (g d)"))])
                A.top = top_a

                otok = [A.bf16(512), A.bf16(512)]
                oacc = [A.f32(256), A.f32(256)]
                tmpS = A.f32(256)
                tmpW = A.f32(256)
                rcs = [[A.f32(4) for _ in range(3)] for _ in range(2)]
                w4s = [[A.f32(4) for _ in range(3)] for _ in range(2)]
                tmp32 = [A.f32(128), A.f32(128)]
                imp = [A.f32(32), A.f32(32)]
                top8 = [A.f32(8), A.f32(8)]
                sel = [A.f32(96), A.f32(96)]
                nselT1 = [A.bf16(512, parts=32), A.bf16(512, parts=32)]
                E_ = lambda j: cb[0:32, CB_E + j * 128:CB_E + (j + 1) * 128]
                steps = [(i, g) for i in range(16) for g in range(2)]
                pending = []

                def Qof(i, g):
                    return QA[g][g * 64:(g + 1) * 64, i, :, :].rearrange("p r q -> p (r q)")

                def norm_gate(Ob, oc, sl, br, i, g):
                    okey = "ps%d" % Ob
                    Ov = ps[Ob][:, 0:4 * oc].rearrange("p (r d) -> p r d", r=4)
                    den = Ov[:, :, 64:65].rearrange("p r d -> p (r d)")
                    rc_, w4_ = rcs[sl][br], w4s[sl][br]
                    kk = "%d%d" % (sl, br)
                    P.dve(lambda e: e.tensor_scalar(rc_, den, 1e-30, None, ALU.max), r=[okey], w=["rc" + kk])
                    P.dve(lambda e: e.reciprocal(rc_, rc_), r=["rc" + kk], w=["rc" + kk])
                    gcol0 = br * 8 + 4 * g
                    P.dve(lambda e: e.tensor_tensor(w4_, rc_, gn[:, i, gcol0:gcol0 + 4], ALU.mult), r=["rc" + kk, "gn"], w=["w4" + kk])
                    return Ov, rc_, w4_.unsqueeze(2).to_broadcast([128, 4, 64]), okey, kk

                def cmp_tile(n):
                    i, g = steps[n]
                    sl = n % 2
                    gs = slice(g * 64, (g + 1) * 64)
                    Ob = next_O()

                    def after():
                        Ov, rc_, w4b, okey, kk = norm_gate(Ob, 97, sl, 0, i, g)
                        oa3 = oacc[sl].rearrange("p (r d) -> p r d", r=4)
                        P.dve(lambda e: e.tensor_tensor(oa3, Ov[:, :, 0:64], w4b, ALU.mult), r=[okey, "w4" + kk], w=["oacc%d" % sl])
                        rcb = rc_.unsqueeze(2).to_broadcast([128, 4, 32])
                        t32, im, t8 = tmp32[sl], imp[sl], top8[sl]
                        se = sel[sl][:, RS[g]]
                        P.dve(lambda e: e.tensor_tensor(t32.rearrange("p (r b) -> p r b", r=4), Ov[:, :, 65:97], rcb, ALU.mult),
                              r=[okey, "rc" + kk], w=["tmp32_%d" % sl])
                        P.dve(lambda e: e.tensor_reduce(im, t32.rearrange("p (r b) -> p b r", r=4), AX.X, ALU.add),
                              r=["tmp32_%d" % sl], w=["imp%d" % sl])
                        P.dve(lambda e: e.tensor_tensor(im, im, cf[:, CF_BONUS + i * 32:CF_BONUS + (i + 1) * 32], ALU.add),
                              r=["imp%d" % sl, "cf"], w=["imp%d" % sl])
                        P.dve(lambda e: e.max(t8, im), r=["imp%d" % sl], w=["top8_%d" % sl])
                        P.dve(lambda e: e.tensor_scalar(se, im, t8[:, 7:8], None, ALU.is_ge), r=["imp%d" % sl, "top8_%d" % sl], w=["sel%d" % sl])
                        P.dve(lambda e: e.tensor_scalar(se, se, -NEG, NEG, ALU.mult, ALU.add), r=["sel%d" % sl], w=["sel%d" % sl])

                    return dict(kp=127, ocols=97, Ob=Ob, init=4, after=after,
                                mms=[(KTc[gs, 0:127], Qof(i, g), ["KTc", "QTn"]),
                                     (cb[0:18, CB_SELC + i * 128:CB_SELC + i * 128 + 127], Gext[g][0:18, :], ["cb", "Gext"])],
                                pv=[(r, Vc[0:127, g, 0:97], True, ["Vc"]) for r in range(4)])

                def cmp_finish(n):
                    i, g = steps[n]
                    sl = n % 2
                    P.tr(ps[7][0:96, 0:128], sel[sl], identf, r=["sel%d" % sl, "cf"], w=["ps7"])
                    dst = QA[g][RS[g], i, :, :]
                    P.dve(lambda e: e.tensor_copy(dst, ps[7][RS[g], 0:128].unsqueeze(1).to_broadcast([32, 4, 128])),
                          r=["ps7"], w=["nsel_%d" % n])

                def main_tiles(n):
                    i, g = steps[n]
                    sl = n % 2
                    gs = slice(g * 64, (g + 1) * 64)
                    Qi = Qof(i, g)
                    tiles = []
                    for br in (1, 2):
                        Ob = next_O()
                        j0 = 0 if br == 1 else max(0, i - 4)
                        KT, V, kkey, vkey = (None, Vs, "KTs", "Vs") if br == 1 else (KTw, Vw, "KTw", "Vw")
                        for j in range(j0, i + 1):
                            dl = i - j
                            if br == 1 and g == 0:
                                mms = [(KA[0][RA[0], j * 128:(j + 1) * 128], QA[0][RA[0], i, :, :].rearrange("p r q -> p (r q)"),
                                        ["KTs", "QTn", "KAe0", "nsel_%d" % n])]
                            elif br == 1:
                                mms = [(KA[1][:, j * 128:(j + 1) * 128], QA[1][:, i, :, :].rearrange("p r q -> p (r q)"),
                                        ["KTs", "QTn", "KAe1", "KAz1", "QAz1", "nsel_%d" % n])]
                            else:
                                mms = [(KT[gs, j * 128:(j + 1) * 128], Qi, [kkey, "QTn"])]
                            if dl == 0:
                                mms.append((ident, Tb["T0", g], ["cb", "Tb"]))
                            elif dl == 1:
                                mms.append((ident, Tb["T128", g], ["cb", "Tb"]))
                            elif dl == 4 and br == 2:
                                mms.append((ident, Tb["Tfar", g], ["cb", "Tb"]))
                            tl = dict(kp=128, ocols=65, Ob=Ob, mms=mms, init=(4 if j == j0 else None),
                                      pv=[(r, V[:, j, g, 0:65], j == i, [vkey]) for r in range(4)])
                            if j == i:
                                def after(br=br, Ob=Ob):
                                    Ov, rc_, w4b, okey, kk = norm_gate(Ob, 65, sl, br, i, g)
                                    tm = tmpS if br == 1 else tmpW
                                    tk = "tmpS" if br == 1 else "tmpW"
                                    P.dve(lambda e: e.tensor_tensor(tm.rearrange("p (r d) -> p r d", r=4), Ov[:, :, 0:64], w4b, ALU.mult),
                                          r=[okey, "w4" + kk], w=[tk])
                                    if br == 1:
                                        P.pool(lambda e: e.tensor_tensor(oacc[sl], oacc[sl], tm, ALU.add), r=["oacc%d" % sl, tk], w=["oacc%d" % sl])
                                    else:
                                        ot = otok[i % 2]
                                        P.pool(lambda e: e.tensor_tensor(ot[:, g * 256:(g + 1) * 256], oacc[sl], tm, ALU.add),
                                               r=["oacc%d" % sl, tk], w=["otok%d_%d" % (i % 2, g)])
                                tl["after"] = after
                            tiles.append(tl)
                    return tiles

                def o_transpose(i):
                    ot = otok[i % 2]
                    pb6 = psb(6)[:, 0:512].rearrange("p (c t) -> p c t", c=4)
                    for c in range(4):
                        P.tr(pb6[:, c, :], ot[:, c * 128:(c + 1) * 128], ident, r=["otok%d_%d" % (i % 2, c // 2), "cb"], w=["ps6"])
                    P.dve(lambda e: e.tensor_copy(oT[:, 0:4, i * 128:(i + 1) * 128], pb6), r=["ps6"], w=["oT"])

                if b == 0:
                    wcfg["late"] = True
                    wstage["st32"] = [A.f32(2048), A.f32(2048)]
                    l16 = A.bf16(4096)
                    wstage["st16"] = [l16[:, 0:2048], l16[:, 2048:4096]]
                run_tiles([cmp_tile(0)])
                cmp_finish(0)
                for n in range(32):
                    i, g = steps[n]
                    tiles = ([cmp_tile(n + 1)] if n + 1 < 32 else []) + main_tiles(n)
                    run_tiles(tiles)
                    if b == 0 and wlate:
                        wlate.pop(0)()
                    for fn in pending:
                        fn()
                    pending = []
                    if n + 1 < 32:
                        cmp_finish(n + 1)
                    if g == 1:
                        pending.append(lambda i=i: o_transpose(i))
                for fn in pending:
                    fn()
                while b == 0 and wlate:
                    wlate.pop(0)()
                P.barrier()
                check(P, "P4a", [("oT", oT.rearrange("p k n -> p (k n)"))])
                A.top = att_top

                wBv = winB_s.rearrange("(k p) n -> p k n", p=128)
                wf = A.bf16(8 * 8).rearrange("p (k n) -> p k n", k=8)
                ee = A.f32(S, parts=8)
                cumn = A.f32(S, parts=8)
                r1 = A.f32(S, parts=8)
                onesf = A.f32(S, parts=8)
                spl = [A.bf16(S, parts=8) for _ in range(6)]
                ones8 = A.bf16(S, parts=8)
                P.pool(lambda e: e.memset(onesf, 1.0), w=["onesf"])
                P.pool(lambda e: e.memset(ones8, 1.0), w=["ones8"])
                P.dma(wf, wBv[:, :, 1536:1544], w=["wf"])
                for tb in range(4):
                    for k in range(8):
                        P.mm(ps[4][0:8, :], wf[:, k, :], uT[:, k, tb * 512:(tb + 1) * 512], start=(k == 0), stop=(k == 7),
                             r=["wf", "uT"], w=["ps4"])
                    P.act(ee[:, tb * 512:(tb + 1) * 512], ps[4][0:8, :], AF.Exp, r=["ps4", "sm", "ee"], w=["ee"], scale=-1.0, bias=sm[0:8, 4:5])
                P.act(ee, ee, AF.Ln, r=["ee", "sm"], w=["ee"], bias=oneb[0:8, :])
                P.dve(lambda e: e.tensor_tensor_scan(cumn, onesf, ee, 0.0, ALU.mult, ALU.add), r=["onesf", "ee"], w=["cumn"])
                P.dve(lambda e: e.tensor_copy(spl[0], cumn), r=["cumn"], w=["spl0"])
                P.dve(lambda e: e.tensor_tensor(r1, cumn, spl[0], ALU.subtract), r=["cumn", "spl0"], w=["r1"])
                P.dve(lambda e: e.tensor_copy(spl[1], r1), r=["r1"], w=["spl1"])
                P.dve(lambda e: e.tensor_tensor(r1, r1, spl[1], ALU.subtract), r=["r1", "spl1"], w=["r1"])
                P.dve(lambda e: e.tensor_copy(spl[2], r1), r=["r1"], w=["spl2"])
                for q in range(3):
                    P.dve(lambda e, q=q: e.tensor_scalar(spl[3 + q], spl[q], -1.0, None, ALU.mult), r=["spl%d" % q], w=["spl%d" % (3 + q)])
                qrows = [spl[3], spl[4], spl[5], ones8, ones8, ones8]
                krows = [ones8, ones8, ones8, spl[0], spl[1], spl[2]]
                allspl = ["spl%d" % q for q in range(6)] + ["ones8"]
                for rr in range(6):
                    P.dma(aug_s[0, rr], qrows[rr], r=allspl, w=["augq%d" % rr])
                    P.dma(aug_s[1, rr], krows[rr], r=allspl, w=["augk%d" % rr])
                P.barrier()
                check(P, "P2b0", [("aug", aug_s.rearrange("a r h n -> (a r h) n"))])
                A.top = att_top

                QTf = A.bf16(8 * S, parts=70).rearrange("p (h n) -> p h n", h=8)
                KTf = A.bf16(8 * S, parts=70).rearrange("p (h n) -> p h n", h=8)
                Vf = A.bf16(16 * 8 * 66).rearrange("p (t h d) -> p t h d", t=16, h=8)
                top_f = A.top
                wq = [A.bf16(8 * 128).rearrange("p (k n) -> p k n", k=8) for _ in range(2)]
                wv = A.bf16(8 * 512).rearrange("p (k n) -> p k n", k=8)
                P.pool(lambda e: e.memset(Vf.rearrange("p t h d -> p (t h d)"), 1.0), w=["Vf"])
                P.dma(QTf[64:70], aug_s[0], r=["augq%d" % rr for rr in range(6)], w=["QTfa"])
                P.dma(KTf[64:70], aug_s[1], r=["augk%d" % rr for rr in range(6)], w=["KTfa"])
                check(P, "P2b1", [("QTf", QTf[64:70].rearrange("p h n -> p (h n)"))])
                for which in range(2):
                    for hp in range(4):
                        s = cnt["w"] % 2
                        cnt["w"] += 1
                        P.dma(wq[s], wBv[:, :, which * 512 + hp * 128:which * 512 + (hp + 1) * 128], w=["wq%d" % s])
                        for hh in range(2):
                            h = hp * 2 + hh
                            for tb in range(4):
                                pb_ = (hh * 4 + tb) % 2
                                for k in range(8):
                                    P.mm(ps[pb_][0:64, :], wq[s][:, k, hh * 64:(hh + 1) * 64], uT[:, k, tb * 512:(tb + 1) * 512],
                                         start=(k == 0), stop=(k == 7), r=["wq%d" % s, "uT"], w=["ps%d" % pb_])
                                if which == 0:
                                    qk_norm(ps[pb_][0:64, :], 64, 512, sm[0:64, 3:4], QTf[0:64, h, tb * 512:(tb + 1) * 512], "ps%d" % pb_, "QTf", Wn)
                                    if h == 0 and tb == 0:
                                        check(P, "P2b2", [("QTf", QTf[0:64, 0, 0:512])])
                                else:
                                    qk_norm(ps[pb_][0:64, :], 64, 512, pk[0:64, PK_GKF:PK_GKF + 1], KTf[0:64, h, tb * 512:(tb + 1) * 512], "ps%d" % pb_, "KTf", Wn)
                P.dma(wv, wBv[:, :, 1024:1536], w=["wv"])
                for tt in range(16):
                    pb_ = 2 + tt % 2
                    for k in range(8):
                        P.mm(ps[pb_][:, :], uT[:, k, tt * 128:(tt + 1) * 128], wv[:, k, :], start=(k == 0), stop=(k == 7),
                             r=["wv", "uT"], w=["ps%d" % pb_])
                    P.act(Vf[:, tt, :, 0:64], ps[pb_][:, :].rearrange("p (h d) -> p h d", h=8), AF.Copy, r=["ps%d" % pb_, "Vf"], w=["Vf"])
                P.barrier()
                check(P, "P2b", [("QTf", QTf.rearrange("p h n -> p (h n)")), ("KTf", KTf.rearrange("p h n -> p (h n)")),
                                 ("Vf", Vf.rearrange("p t h d -> p (t h d)"))])
                A.top = top_f

                oftok = A.bf16(4 * 512).rearrange("p (t n) -> p t n", t=4)
                rcf = [A.f32(4), A.f32(4)]
                for I in range(4):
                    tiles = []
                    for h in range(8):
                        Ob = next_O()
                        Qb = QTf[0:70, h, I * 512:(I + 1) * 512]
                        nj = 4 * I + 4
                        for j in range(nj):
                            mms = [(KTf[0:70, h, j * 128:(j + 1) * 128], Qb, ["KTf", "QTf", "KTfa", "QTfa"])]
                            if j >= 4 * I:
                                m = j - 4 * I
                                mms.append((ident, cb[:, CB_MASK + m * 512:CB_MASK + (m + 1) * 512], ["cb"]))
                            pv = [(t, Vf[:, j, h, 0:65], j == 4 * I + t, ["Vf"]) for t in range(4) if 4 * I + t >= j]
                            tl = dict(kp=128, ocols=65, Ob=Ob, mms=mms, pv=pv, init=(4 if j == 0 else None))
                            if j == nj - 1:
                                def after(h=h, Ob=Ob):
                                    okey = "ps%d" % Ob
                                    Ov = ps[Ob][:, 0:260].rearrange("p (r d) -> p r d", r=4)
                                    den = Ov[:, :, 64:65].rearrange("p r d -> p (r d)")
                                    rc_ = rcf[h % 2]
                                    kk = "rcf%d" % (h % 2)
                                    P.dve(lambda e: e.tensor_scalar(rc_, den, 1e-30, None, ALU.max), r=[okey], w=[kk])
                                    P.dve(lambda e: e.reciprocal(rc_, rc_), r=[kk], w=[kk])
                                    rcb = rc_.unsqueeze(2).to_broadcast([128, 4, 64])
                                    P.dve(lambda e: e.tensor_tensor(oftok[:, :, h * 64:(h + 1) * 64], Ov[:, :, 0:64], rcb, ALU.mult),
                                          r=[okey, kk], w=["oftok%d" % h])
                                tl["after"] = after
                            tiles.append(tl)
                    run_tiles(tiles)
                    for t in range(4):
                        pb6 = psb(6)[:, 0:512].rearrange("p (c t) -> p c t", c=4)
                        for c in range(4):
                            P.tr(pb6[:, c, :], oftok[:, t, c * 128:(c + 1) * 128], ident, r=["oftok%d" % (2 * c), "oftok%d" % (2 * c + 1), "cb"], w=["ps6"])
                        tt = 4 * I + t
                        P.dve(lambda e, tt=tt, pb6=pb6: e.tensor_copy(oT[:, 4:8, tt * 128:(tt + 1) * 128], pb6), r=["ps6", "oT"], w=["oT"])
                P.barrier()
                check(P, "P4b", [("oT", oT.rearrange("p k n -> p (k n)"))])
                A.top = seq_top

                won = A.bf16(4 * 1024).rearrange("p (c n) -> p c n", c=4)
                wof = A.bf16(4 * 1024).rearrange("p (c n) -> p c n", c=4)
                wo = A.bf16(8 * 1024).rearrange("p (c n) -> p c n", c=8)
                gma = [A.bf16(8 * 128).rearrange("p (k n) -> p k n", k=8) for _ in range(2)]
                gmb = [A.bf16(8 * 128).rearrange("p (k n) -> p k n", k=8) for _ in range(2)]
                mT = A.bf16(8 * 512).rearrange("p (f n) -> p f n", f=8)
                sa = [A.f32(512), A.f32(512)]
                sb_ = [A.f32(512), A.f32(512)]
                m1 = [A.f32(512), A.f32(512)]
                m2 = [A.f32(512), A.f32(512)]
                xr = [A.f32(1024), A.f32(1024)]
                P.dma(won, won_s.rearrange("(c p) n -> p c n", p=128), w=["won"])
                P.dma(wof, wof_s.rearrange("(c p) n -> p c n", p=128), w=["wof"])
                P.dma(wo, wout_s.rearrange("(c p) n -> p c n", p=128), w=["wo"])
                wGv = winG_s.rearrange("(k p) n -> p k n", p=128)
                for tb in range(4):
                    tbs = slice(tb * 512, (tb + 1) * 512)
                    for f in range(8):
                        s = f % 2
                        P.dma(gma[s], wGv[:, :, f * 128:(f + 1) * 128], w=["gma%d" % s])
                        P.dma(gmb[s], wGv[:, :, 1024 + f * 128:1024 + (f + 1) * 128], w=["gmb%d" % s])
                        b0 = s * 4
                        for c in range(4):
                            P.mm(ps[b0][:, :], won[:, c, f * 128:(f + 1) * 128], oT[:, c, tbs], start=(c == 0), stop=(c == 3),
                                 r=["won", "oT"], w=["ps%d" % b0])
                        for c in range(4):
                            P.mm(ps[b0 + 1][:, :], wof[:, c, f * 128:(f + 1) * 128], oT[:, 4 + c, tbs], start=(c == 0), stop=(c == 3),
                                 r=["wof", "oT"], w=["ps%d" % (b0 + 1)])
                        for k in range(8):
                            P.mm(ps[b0 + 2][:, :], gma[s][:, k, :], uT[:, k, tbs], start=(k == 0), stop=(k == 7),
                                 r=["gma%d" % s, "uT"], w=["ps%d" % (b0 + 2)])
                        for k in range(8):
                            P.mm(ps[b0 + 3][:, :], gmb[s][:, k, :], uT[:, k, tbs], start=(k == 0), stop=(k == 7),
                                 r=["gmb%d" % s, "uT"], w=["ps%d" % (b0 + 3)])
                        P.act(sa[s], ps[b0 + 2][:, :], AF.Sigmoid, r=["ps%d" % (b0 + 2)], w=["sa%d" % s])
                        P.act(sb_[s], ps[b0 + 3][:, :], AF.Sigmoid, r=["ps%d" % (b0 + 3)], w=["sb%d" % s])
                        P.dve(lambda e, s=s, b0=b0: e.tensor_tensor(m1[s], ps[b0][:, :], sa[s], ALU.mult), r=["ps%d" % b0, "sa%d" % s], w=["m1%d" % s])
                        P.dve(lambda e, s=s, b0=b0: e.tensor_tensor(m2[s], ps[b0 + 1][:, :], sb_[s], ALU.mult), r=["ps%d" % (b0 + 1), "sb%d" % s], w=["m2%d" % s])
                        P.pool(lambda e, s=s, f=f: e.tensor_tensor(mT[:, f, :], m1[s], m2[s], ALU.add), r=["m1%d" % s, "m2%d" % s], w=["mT%d" % f])
                    for t in range(4):
                        tt = tb * 4 + t
                        xs_ = xr[tt % 2]
                        xk = "xr%d" % (tt % 2)
                        P.dma(xs_, x1_s[b, tt * 128:(tt + 1) * 128, :], r=["x1s_%d" % tt], w=[xk])
                        for dh in range(2):
                            pd = dh
                            for f in range(8):
                                P.mm(ps[pd][:, :], mT[:, f, t * 128:(t + 1) * 128], wo[:, f, dh * 512:(dh + 1) * 512],
                                     start=(f == 0), stop=(f == 7), r=["mT%d" % f, "wo"], w=["ps%d" % pd])
                            xh = xs_[:, dh * 512:(dh + 1) * 512]
                            P.dve(lambda e, pd=pd, xh=xh: e.tensor_tensor(xh, ps[pd][:, :], xh, ALU.add), r=["ps%d" % pd, xk], w=[xk])
                        P.dma(x1_s[b, tt * 128:(tt + 1) * 128, :], xs_, r=[xk], w=["x1s_%d" % tt], q="pool")
                P.barrier()
                check(P, "P5", [("x2", x1_s[0])])
                A.top = base_top

                def load_x2(xt, key, tt, b=b):
                    P.dma(xt, x1_s[b, tt * 128:(tt + 1) * 128, :], r=["x1s_%d" % tt], w=[key])

                def store_y(xt, key, tt, b=b):
                    P.dma(y_d[b, tt * 128:(tt + 1) * 128, :], xt, r=[key], w=["y_%d_%d" % (b, tt)], q="pool")

                ffn(1, b, load_x2, store_y, False, None)
                A.top = base_top

        try:
            body()
        except _Stop:
            pass
        P.barrier()
        P.finalize(st)
    return nc


_NC_CACHE = {}


def kernel(**inputs):
    inp = {k: np.ascontiguousarray(np.asarray(v)) for k, v in inputs.items()}
    n = 8
    nseq = 2
    if "nc" not in _NC_CACHE:
        _NC_CACHE["nc"] = build_nc(nseq)
    nc = _NC_CACHE["nc"]
    cb, cf = _host_consts()
    pk = _pack_params(inp)
    shared = {
        "ffn1_w_up": inp["ffn1_w_up"][0], "ffn2_w_up": inp["ffn2_w_up"][0],
        "ffn1_w_down": inp["ffn1_w_down"][0], "ffn2_w_down": inp["ffn2_w_down"][0],
        "w_in": inp["w_in"][0],
        "cmp_k_w1": inp["cmp_k_w1"][0], "cmp_v_w1": inp["cmp_v_w1"][0],
        "cmp_k_w2": inp["cmp_k_w2"][0], "cmp_v_w2": inp["cmp_v_w2"][0],
        "w_o_nsa": inp["w_o_nsa"][0], "w_o_fox": inp["w_o_fox"][0], "w_out": inp["w_out"][0],
        "cb": cb, "cf": cf, "pk": pk,
    }
    x = inp["x"]
    in_maps = []
    for c in range(n):
        m = dict(shared)
        m["x"] = np.ascontiguousarray(x[c * nseq:(c + 1) * nseq])
        in_maps.append(m)
    res = run_bass_kernel_spmd(nc, in_maps, core_ids=list(range(n)))
    out = np.concatenate([np.asarray(r["y"]) for r in res.results], axis=0)
    return out.astype(np.float32)
```

```python
import math
from contextlib import ExitStack
import numpy as np
import ml_dtypes
import concourse.bass as bass
import concourse.mybir as mybir
from concourse.bass_utils import run_bass_kernel_spmd

F32 = mybir.dt.float32
BF16 = mybir.dt.bfloat16
AF = mybir.ActivationFunctionType
ALU = mybir.AluOpType
AX = mybir.AxisListType

ENGS = ["pe", "act", "dve", "pool", "sp"]
NDSEM = 12
S = 2048
D = 1024
DFF = 2816
NCH = 22
NEG = -30000.0
DIN = 4896


class Prog:
    def __init__(self, nc):
        self.nc = nc
        self.ops = []

    def add(self, eng, fn, r=(), w=(), dma=False):
        self.ops.append({"eng": eng, "fn": fn, "r": list(r), "w": list(w), "dma": dma})

    def barrier(self):
        for e in ENGS:
            self.ops.append({"eng": e, "fn": None, "r": [], "w": [], "dma": False, "bar": True})

    def mm(self, out, lhsT, rhs, start=True, stop=True, r=(), w=()):
        self.add("pe", lambda e: e.matmul(out, lhsT, rhs, start=start, stop=stop), r, w)

    def tr(self, out, in_, ident, r=(), w=()):
        self.add("pe", lambda e: e.transpose(out, in_, ident), r, w)

    def act(self, out, in_, func, r=(), w=(), **kw):
        self.add("act", lambda e: e.activation(out, in_, func, **kw), r, w)

    def dve(self, fn, r=(), w=()):
        self.add("dve", fn, r, w)

    def pool(self, fn, r=(), w=()):
        self.add("pool", fn, r, w)

    def dma(self, out, in_, r=(), w=(), q="sp"):
        self.add(q, lambda e: e.dma_start(out=out, in_=in_), r, w, dma=True)

    def finalize(self, stack):
        nc = self.nc
        ops = self.ops
        n = len(ops)
        last_w, readers, eng_last, last_dma, dma_n, dma_cnt = {}, {}, {}, {}, {}, {}
        deps = [None] * n
        for i, op in enumerate(ops):
            d = set()
            if op.get("bar"):
                d.update(eng_last.values())
                d.update(last_dma.values())
            else:
                for k in op["r"]:
                    if k in last_w:
                        d.add(last_w[k])
                for k in op["w"]:
                    if k in last_w:
                        d.add(last_w[k])
                    d.update(readers.get(k, ()))
                for k in op["w"]:
                    last_w[k] = i
                    readers[k] = []
                for k in op["r"]:
                    readers.setdefault(k, []).append(i)
            if op["dma"]:
                q = op["eng"]
                m = dma_n.get(q, 0)
                dma_n[q] = m + 1
                key = ("d", q, m % NDSEM)
                dma_cnt[key] = dma_cnt.get(key, 0) + 1
                op["sem"] = key
                op["val"] = 16 * dma_cnt[key]
                if key in last_dma:
                    d.add(last_dma[key])
                last_dma[key] = i
            elif not op.get("bar"):
                eng_last[op["eng"]] = i
            d.discard(i)
            red = {}
            for j in d:
                pj = ops[j]
                key = pj["sem"] if pj["dma"] else ("e", pj["eng"])
                if red.get(key, -1) < j:
                    red[key] = j
            deps[i] = set(red.values())
        signal = [False] * n
        for i in range(n):
            for j in deps[i]:
                signal[j] = True
        cnt = {e: 0 for e in ENGS}
        for i, op in enumerate(ops):
            if not op["dma"] and signal[i]:
                cnt[op["eng"]] += 1
                op["sem"] = ("e", op["eng"])
                op["val"] = cnt[op["eng"]]
        semh = {}
        for e in ENGS:
            semh[("e", e)] = stack.enter_context(nc.semaphore("s_" + e))
        for q in dma_n:
            for s in range(NDSEM):
                semh[("d", q, s)] = stack.enter_context(nc.semaphore("d_%s_%d" % (q, s)))
        per_eng = {e: [] for e in ENGS}
        for i, op in enumerate(ops):
            per_eng[op["eng"]].append(i)

        def emit(e, eng):
            known = {}
            for i in per_eng[e]:
                op = ops[i]
                need = {}
                for j in deps[i]:
                    pj = ops[j]
                    if (not pj["dma"]) and pj["eng"] == e and e == "pe":
                        continue
                    key = pj["sem"]
                    if need.get(key, 0) < pj["val"]:
                        need[key] = pj["val"]
                for key, val in need.items():
                    if known.get(key, 0) < val:
                        eng.wait_ge(semh[key], val)
                        known[key] = val
                if op["fn"] is not None:
                    inst = op["fn"](eng)
                    if op["dma"]:
                        inst.then_inc(semh[op["sem"]], 16)
                    elif signal[i]:
                        inst.then_inc(semh[op["sem"]], 1)

        with nc.Block() as block:
            @block.tensor
            def _(eng):
                emit("pe", eng)

            @block.scalar
            def _(eng):
                emit("act", eng)

            @block.vector
            def _(eng):
                emit("dve", eng)

            @block.gpsimd
            def _(eng):
                emit("pool", eng)

            @block.sync
            def _(eng):
                emit("sp", eng)


class Alloc:
    def __init__(self, big, nwords):
        self.big = big
        self.top = 0
        self.n = nwords

    def f32(self, cols, parts=128):
        off = self.top
        self.top += cols
        assert self.top <= self.n, ("sbuf overflow", self.top)
        return self.big[0:parts, off:off + cols]

    def bf16(self, cols, parts=128):
        assert cols % 2 == 0
        w = cols // 2
        off = self.top
        self.top += w
        assert self.top <= self.n, ("sbuf overflow", self.top)
        return self.big[0:parts, off:off + w].bitcast(BF16)


CB_ID, CB_BO, CB_E, CB_MASK, CB_OV, CB_SELC, CB_ONES, NCB = 0, 128, 256, 2304, 4352, 4384, 6432, 6560
CF_ID, CF_BONUS, CF_OH, CF_FAR, CF_ONES33, NCF = 0, 128, 640, 1024, 1152, 1280
PK_G1, PK_GM, PK_G2, PK_GQN, PK_GKC, PK_GKS, PK_GKW, PK_GQF, PK_GKF, PK_BF, PK_TAB, PK_POSK, PK_POSV, NPK = \
    0, 8, 16, 24, 25, 26, 27, 28, 29, 30, 32, 40, 72, 104


def _t5_bucket_np(n):
    n = np.maximum(n, 0)
    nf = np.maximum(n, 1).astype(np.float32)
    large = 16 + (np.log(nf / np.float32(16)) / np.float32(math.log(128 / 16)) * np.float32(16)).astype(np.int32)
    large = np.minimum(large, 31)
    return np.where(n < 16, n, large)


def _host_consts():
    cb = np.zeros((128, NCB), np.float32)
    cb[:, CB_ID:CB_ID + 128] = np.eye(128)
    p = np.arange(128)
    cb[:, CB_BO:CB_BO + 128] = (p[:, None] // 64 == p[None, :] // 64)
    for j in range(16):
        for kl in range(128):
            cb[2 * j + kl // 64, CB_E + j * 128 + kl] = 1.0
    for m in range(4):
        qc = np.arange(512)
        cb[:, CB_MASK + m * 512:CB_MASK + (m + 1) * 512] = np.where(qc[None, :] >= 128 * m + p[:, None], 0.0, NEG)
    ci = np.arange(127)[:, None] * 16
    sj = np.arange(32)[None, :] * 64
    cb[0:127, CB_OV:CB_OV + 32] = ((ci <= sj + 63) & (ci + 31 >= sj))
    for i in range(16):
        for c in range(127):
            cp = c - 8 * i
            if -9 <= cp <= 6:
                cb[6 - cp, CB_SELC + i * 128 + c] = 1.0
            elif cp < -9:
                cb[16, CB_SELC + i * 128 + c] = 1.0
            else:
                cb[17, CB_SELC + i * 128 + c] = 1.0
    cb[:, CB_ONES:CB_ONES + 128] = 1.0
    cf = np.zeros((128, NCF), np.float32)
    cf[:, CF_ID:CF_ID + 128] = np.eye(128)
    for i in range(16):
        t = 128 * i + p
        cur = t // 64
        blk = np.arange(32)
        forced = (blk[None, :] == 0) | (blk[None, :] == cur[:, None]) | (blk[None, :] == cur[:, None] - 1)
        cf[:, CF_BONUS + i * 32:CF_BONUS + (i + 1) * 32] = np.where(blk[None, :] <= cur[:, None], 1.0e4 * forced, -1.0e30)
    m = np.arange(384)
    dd = m - 127
    bk = _t5_bucket_np(dd)
    for b in range(32):
        cf[b, CF_OH:CF_OH + 384] = ((dd >= 0) & (bk == b))
    cf[32, CF_OH:CF_OH + 384] = (dd < 0)
    cf[:, CF_FAR:CF_FAR + 128] = (p[None, :] < p[:, None])
    cf[0:33, CF_ONES33:CF_ONES33 + 128] = 1.0
    return cb.astype(ml_dtypes.bfloat16), cf


def _pack_params(inp):
    pk = np.zeros((128, NPK), np.float32)
    pk[:, PK_G1:PK_G1 + 8] = inp["ffn1_norm"][0].reshape(8, 128).T
    pk[:, PK_GM:PK_GM + 8] = inp["mix_norm"][0].reshape(8, 128).T
    pk[:, PK_G2:PK_G2 + 8] = inp["ffn2_norm"][0].reshape(8, 128).T
    pk[:, PK_GQN] = np.tile(inp["nsa_q_gain"][0], 2)
    pk[:, PK_GKC] = np.tile(inp["nsa_k_gain"][0, 0], 2)
    pk[:, PK_GKS] = np.tile(inp["nsa_k_gain"][0, 1], 2)
    pk[:, PK_GKW] = np.tile(inp["nsa_k_gain"][0, 2], 2)
    pk[:, PK_GQF] = np.tile(inp["fox_q_gain"][0], 2)
    pk[:, PK_GKF] = np.tile(inp["fox_k_gain"][0], 2)
    pk[0:8, PK_BF] = inp["b_forget"][0]
    pk[0:32, PK_TAB:PK_TAB + 8] = inp["rel_bias_table"]
    pk[32, PK_TAB:PK_TAB + 8] = NEG
    pk[:, PK_POSK:PK_POSK + 32] = np.tile(inp["cmp_pos_k"][0].T, (2, 1))
    pk[:, PK_POSV:PK_POSV + 32] = np.tile(inp["cmp_pos_v"][0].T, (2, 1))
    return pk


class _Stop(Exception):
    pass


def build_nc(nseq=2, stop=None):
    nc = bass.Bass("TRN2", target_bir_lowering=False)
    dbg_n = [0]

    def dbgout(P, name, src):
        o = nc.dram_tensor("dbg_" + name, list(src.shape), src.dtype, kind="ExternalOutput").ap()
        P.dma(o, src)

    def check(P, tag, outs):
        if stop == tag:
            P.barrier()
            for nm, src in outs:
                dbgout(P, nm, src)
            raise _Stop()

    def din(name, shape, dt=F32):
        return nc.dram_tensor(name, list(shape), dt, kind="ExternalInput").ap()

    def dscr(name, shape, dt=BF16):
        return nc.dram_tensor(name, list(shape), dt, kind="Internal").ap()

    x_d = din("x", [nseq, S, D])
    y_d = nc.dram_tensor("y", [nseq, S, D], F32, kind="ExternalOutput").ap()
    w_up_d = [din("ffn1_w_up", [D, 2 * DFF]), din("ffn2_w_up", [D, 2 * DFF])]
    w_dn_d = [din("ffn1_w_down", [DFF, D]), din("ffn2_w_down", [DFF, D])]
    w_in_d = din("w_in", [D, DIN])
    cw1_d = [din("cmp_k_w1", [2048, 128]), din("cmp_v_w1", [2048, 128])]
    cw2_d = [din("cmp_k_w2", [128, 64]), din("cmp_v_w2", [128, 64])]
    won_d = din("w_o_nsa", [512, D])
    wof_d = din("w_o_fox", [512, D])
    wout_d = din("w_out", [D, D])
    cb_d = din("cb", [128, NCB], BF16)
    cf_d = din("cf", [128, NCF])
    pk_d = din("pk", [128, NPK])

    wup_s = [dscr("wup1s", [D, 2 * DFF]), dscr("wup2s", [D, 2 * DFF])]
    wdn_s = [dscr("wdn1s", [DFF, D]), dscr("wdn2s", [DFF, D])]
    winA_s = dscr("winAs", [D, 1304])
    winB_s = dscr("winBs", [D, 1544])
    winG_s = dscr("winGs", [D, 2048])
    cw1_s = [dscr("cw1ks", [2048, 128]), dscr("cw1vs", [2048, 128])]
    cw2_s = [dscr("cw2ks", [128, 64]), dscr("cw2vs", [128, 64])]
    won_s = dscr("wons", [512, D])
    wof_s = dscr("wofs", [512, D])
    wout_s = dscr("wouts", [D, D])
    x1_s = dscr("x1s", [nseq, S, D], F32)
    buf_s = dscr("bufs", [8, 128, 384], F32)
    aug_s = dscr("augs", [2, 6, 8, S], BF16)

    st = ExitStack()
    with st:
        NW = 52800
        big = st.enter_context(nc.sbuf_tensor("big", [128, NW], F32))
        ps = [st.enter_context(nc.psum_tensor("ps%d" % i, [128, 512], F32)) for i in range(8)]
        P = Prog(nc)
        A = Alloc(big, NW)
        def body():

            def psb(i):
                return ps[i][:, :].bitcast(BF16)

            cb = A.bf16(NCB)
            cf = A.f32(NCF)
            pk = A.f32(NPK)
            sm = A.f32(8)
            Tb = {}
            for nm in ("T0", "T128", "Tfar"):
                for g in range(2):
                    Tb[nm, g] = A.bf16(512)
            Gext = [A.bf16(512), A.bf16(512)]
            rc = A.f32(4)
            w4 = A.f32(4)
            P.dma(cb, cb_d, w=["cb"])
            P.dma(cf, cf_d, w=["cf"])
            P.dma(pk, pk_d, w=["pk"])
            P.pool(lambda e: e.memset(sm[:, 0:1], 1e-6), w=["sm"])
            P.pool(lambda e: e.memset(sm[:, 1:2], 1.0), w=["sm"])
            P.dve(lambda e: e.tensor_scalar(sm[:, 2:3], pk[:, PK_GQN:PK_GQN + 1], 0.125, None, ALU.mult), r=["pk", "sm"], w=["sm"])
            P.dve(lambda e: e.tensor_scalar(sm[:, 3:4], pk[:, PK_GQF:PK_GQF + 1], 0.125, None, ALU.mult), r=["pk", "sm"], w=["sm"])
            P.dve(lambda e: e.tensor_scalar(sm[:, 4:5], pk[:, PK_BF:PK_BF + 1], -1.0, None, ALU.mult), r=["pk", "sm"], w=["sm"])
            ident = cb[:, CB_ID:CB_ID + 128]
            blockones = cb[:, CB_BO:CB_BO + 128]
            identf = cf[:, CF_ID:CF_ID + 128]
            epsb = sm[:, 0:1]
            oneb = sm[:, 1:2]
            base_top = A.top

            wpending = []
            wstage = {}
            wcnt = [0]

            def convert(src, R, c0, c1, segs):
                assert c1 - c0 <= 2048
                for rc in range(R // 128):
                    def piece(rc=rc):
                        st32, st16 = wstage["st32"], wstage["st16"]
                        s = wcnt[0] % 2
                        wd = c1 - c0
                        k32, k16 = ("stg0", "stg1")[s], ("stg2a", "stg2b")[s]
                        P.dma(st32[s][:, 0:wd], src[rc * 128:(rc + 1) * 128, c0:c1], w=[k32])
                        o_, i_ = st16[s][:, 0:wd], st32[s][:, 0:wd]
                        if wcfg["late"]:
                            P.pool(lambda e: e.tensor_copy(o_, i_), r=[k32], w=[k16])
                        elif wcnt[0] % 2 == 0:
                            P.dve(lambda e: e.tensor_copy(o_, i_), r=[k32, "stg2"], w=[k16])
                        else:
                            P.act(o_, i_, AF.Copy, r=[k32, "stg2"], w=[k16])
                        for (sc, wdt, dst, dc) in segs:
                            P.dma(dst[rc * 128:(rc + 1) * 128, dc:dc + wdt], st16[s][:, sc - c0:sc - c0 + wdt], r=[k16], q="pool")
                        wcnt[0] += 1
                    wpending.append(piece)

            segsA = []
            for g in range(2):
                for r in range(4):
                    segsA.append(((4 * g + r) * 64, 64, winA_s, r * 128 + g * 64))
            segsA += [(512, 128, winA_s, 512), (640, 128, winA_s, 640), (768, 128, winA_s, 768),
                      (1024, 128, winA_s, 896), (896, 128, winA_s, 1024), (1152, 128, winA_s, 1152),
                      (1280, 24, winA_s, 1280)]
            convert(w_in_d, D, 0, 1304, segsA)
            for kv in range(2):
                convert(cw1_d[kv], 2048, 0, 128, [(0, 128, cw1_s[kv], 0)])
                convert(cw2_d[kv], 128, 0, 64, [(0, 64, cw2_s[kv], 0)])
            n_early = len(wpending)
            convert(w_in_d, D, 1304, 2848, [(1304, 1544, winB_s, 0)])
            convert(w_in_d, D, 2848, 4896, [(2848, 2048, winG_s, 0)])
            convert(won_d, 512, 0, 1024, [(0, 1024, won_s, 0)])
            convert(wof_d, 512, 0, 1024, [(0, 1024, wof_s, 0)])
            convert(wout_d, D, 0, 1024, [(0, 1024, wout_s, 0)])
            wlate = wpending[n_early:]
            del wpending[n_early:]
            wcfg = {"late": False}

            Xt = A.f32(8 * 384, parts=33)
            rep = A.f32(8 * 384)
            for h in range(8):
                P.dve(lambda e, h=h: e.tensor_scalar(Xt[:, h * 384:(h + 1) * 384], cf[0:33, CF_OH:CF_OH + 384],
                                                     pk[0:33, PK_TAB + h:PK_TAB + h + 1], None, ALU.mult),
                      r=["cf", "pk"], w=["Xt"])
            for q in range(6):
                P.mm(ps[q][:, :], cf[0:33, CF_ONES33:CF_ONES33 + 128], Xt[:, q * 512:(q + 1) * 512], r=["cf", "Xt"], w=["ps%d" % q])
                P.act(rep[:, q * 512:(q + 1) * 512], ps[q][:, :], AF.Copy, r=["ps%d" % q], w=["rep"])
            b31c = A.f32(8)
            for h in range(8):
                P.dve(lambda e, h=h: e.tensor_copy(b31c[:, h:h + 1], rep[:, h * 384 + 327:h * 384 + 328]), r=["rep", "b31c"], w=["b31c"])
            for h in range(8):
                P.dve(lambda e, h=h: e.tensor_scalar(rep[:, h * 384:(h + 1) * 384], rep[:, h * 384:(h + 1) * 384], b31c[:, h:h + 1], None,
                                                     ALU.subtract), r=["rep", "b31c"], w=["rep"])
            P.dma(buf_s.rearrange("h k m -> k h m"), rep.rearrange("p (h m) -> p h m", h=8), r=["rep"], w=["bufs"])
            tst = A.f32(512)
            for g in range(2):
                for nm, delta in (("T0", 0), ("T128", 128)):
                    src = bass.AP(buf_s.tensor, delta + 127 + 4 * g * 49152, [[383, 128], [49152, 4], [1, 128]])
                    P.dma(tst.rearrange("p (r q) -> p r q", r=4), src, r=["bufs"], w=["tst"])
                    P.dve(lambda e, nm=nm, g=g: e.tensor_copy(Tb[nm, g], tst), r=["tst"], w=["Tb"])
                tst2 = A.f32(128)
                P.dve(lambda e, tst2=tst2: e.tensor_scalar(tst2, cf[:, CF_FAR:CF_FAR + 128], -NEG, NEG, ALU.mult, ALU.add),
                      r=["cf", "tst2"], w=["tst2"])
                P.dve(lambda e, g=g, tst2=tst2: e.tensor_copy(Tb["Tfar", g].rearrange("p (r q) -> p r q", r=4),
                                                              tst2.unsqueeze(1).to_broadcast([128, 4, 128])),
                      r=["tst2"], w=["Tb"])
                zf = A.f32(512)
                negf = A.f32(512)
                P.pool(lambda e, zf=zf: e.memset(zf, 0.0), w=["zf"])
                P.pool(lambda e, negf=negf: e.memset(negf, NEG), w=["negf"])
                gst = A.f32(512, parts=18)
                srcg = bass.AP(buf_s.tensor, 4 * g * 49152, [[400, 16], [49152, 4], [1, 128]])
                P.dma(gst[0:16, :].rearrange("p (r q) -> p r q", r=4), srcg, r=["bufs"], w=["gst"])
                P.dma(gst[16:17, :], zf[0:1, :], r=["zf", "gst"], w=["gst"])
                P.dma(gst[17:18, :], negf[0:1, :], r=["negf", "gst"], w=["gst"])
                P.dve(lambda e, g=g, gst=gst: e.tensor_copy(Gext[g][0:18, :], gst), r=["gst"], w=["Gext"])
            P.barrier()
            check(P, "W", [("T0", Tb["T0", 0]), ("T128", Tb["T128", 1]), ("Tfar", Tb["Tfar", 0]),
                           ("Gext", Gext[0][0:18, :]), ("winA", winA_s[0:128, :]), ("wout", wout_s[896:1024, :])])
            A.top = base_top

            cnt = {"xs": 0, "S": 0, "P": 0, "O": 0, "w": 0, "n": 0}

            def rms_T(xt, xkey, gcol, dst_fn, dkey, W, defer=False):
                s = cnt["xs"] % 2
                cnt["xs"] += 1
                junk, ss, rstd, xs = W["junk"], W["ss"][s], W["rstd"][s], W["xs"][s]
                P.act(junk, xt, AF.Square, r=[xkey], w=["junk", "ss%d" % s], accum_out=ss)
                P.act(rstd, ss, AF.Ln, r=["ss%d" % s, "sm"], w=["rstd%d" % s], scale=1.0 / D, bias=epsb)
                P.act(rstd, rstd, AF.Exp, r=["rstd%d" % s], w=["rstd%d" % s], scale=-0.5)
                P.dve(lambda e: e.tensor_scalar(xs, xt, rstd, None, ALU.mult), r=[xkey, "rstd%d" % s], w=["xs%d" % s])

                def part_b():
                    pb = psb(6).rearrange("p (k t) -> p k t", k=8)
                    for k in range(8):
                        P.tr(pb[:, k, :], xs[:, k * 128:(k + 1) * 128], ident, r=["xs%d" % s, "cb"], w=["ps6"])
                    gT = pk[:, gcol:gcol + 8].unsqueeze(2).to_broadcast([128, 8, 128])
                    P.dve(lambda e: e.tensor_tensor(dst_fn(), pb, gT, ALU.mult), r=["ps6", "pk"], w=[dkey])
                if defer:
                    return part_b
                part_b()
                return None

            def ffn(f, b, load_x, store_out, make_u, uT):
                top0 = A.top
                W = {"junk": A.bf16(1024), "ss": [A.f32(1), A.f32(1)], "rstd": [A.f32(1), A.f32(1)],
                     "xs": [A.bf16(1024), A.bf16(1024)]}
                wd = A.bf16(NCH * 1024).rearrange("p (c n) -> p c n", c=NCH)
                wg = [A.bf16(8 * 256).rearrange("p (k n) -> p k n", k=8) for _ in range(2)]
                wu = [A.bf16(8 * 256).rearrange("p (k n) -> p k n", k=8) for _ in range(2)]
                hT = A.bf16(NCH * 512).rearrange("p (c n) -> p c n", c=NCH)
                xnT = A.bf16(8 * 512).rearrange("p (k n) -> p k n", k=8)
                xin = [A.f32(1024) for _ in range(4)]
                sg = [A.f32(512), A.f32(512)]
                gcol = PK_G1 if f == 0 else PK_G2
                carry = [None]
                first = (b == 0)
                wdv = wdn_s[f].rearrange("(c p) n -> p c n", p=128)
                wupv = wup_s[f].rearrange("(k p) n -> p k n", p=128)
                if first:
                    stg = [A.f32(2048), A.f32(2048), A.f32(2048)]
                    wup32 = w_up_d[f].rearrange("(k p) n -> p k n", p=128)
                    wdn32 = w_dn_d[f].rearrange("(c p) n -> p c n", p=128)
                else:
                    P.dma(wd, wdv, r=["wdnS%d_%d" % (f, q) for q in range(11)], w=["wd"])
                for tb in range(4):
                    for t in range(4):
                        tt = tb * 4 + t
                        load_x(xin[t], "xin%d" % t, tt)
                        rms_T(xin[t], "xin%d" % t, gcol, lambda t=t: xnT[:, :, t * 128:(t + 1) * 128], "xnT", W)
                        if t == 0 and carry[0] is not None:
                            carry[0]()
                            carry[0] = None
                    for cp in range(11):
                        s = cnt["w"] % 2
                        cnt["w"] += 1
                        gk, uk = "wupS%d_g%d" % (f, cp), "wupS%d_u%d" % (f, cp)
                        if first and tb == 0:
                            P.dma(stg[0].rearrange("p (k n) -> p k n", k=8), wup32[:, :, cp * 256:(cp + 1) * 256], w=["stg0"])
                            P.pool(lambda e, s=s: e.tensor_copy(wg[s], stg[0].rearrange("p (k n) -> p k n", k=8)), r=["stg0"], w=["wg%d" % s])
                            P.dma(wupv[:, :, cp * 256:(cp + 1) * 256], wg[s], r=["wg%d" % s], w=[gk], q="pool")
                            P.dma(stg[1].rearrange("p (k n) -> p k n", k=8), wup32[:, :, DFF + cp * 256:DFF + (cp + 1) * 256], w=["stg1"])
                            P.dve(lambda e, s=s: e.tensor_copy(wu[s], stg[1].rearrange("p (k n) -> p k n", k=8)), r=["stg1"], w=["wu%d" % s])
                            P.dma(wupv[:, :, DFF + cp * 256:DFF + (cp + 1) * 256], wu[s], r=["wu%d" % s], w=[uk], q="pool")
                            P.dma(stg[2].rearrange("p (c n) -> p c n", c=2), wdn32[:, 2 * cp:2 * cp + 2, :], w=["stg2"])
                            P.act(wd[:, 2 * cp:2 * cp + 2, :], stg[2].rearrange("p (c n) -> p c n", c=2), AF.Copy, r=["stg2", "wd"], w=["wd"])
                            P.dma(wdv[:, 2 * cp:2 * cp + 2, :], wd[:, 2 * cp:2 * cp + 2, :], r=["wd"], w=["wdnS%d_%d" % (f, cp)], q="pool")
                        else:
                            P.dma(wg[s], wupv[:, :, cp * 256:(cp + 1) * 256], r=[gk], w=["wg%d" % s])
                            P.dma(wu[s], wupv[:, :, DFF + cp * 256:DFF + (cp + 1) * 256], r=[uk], w=["wu%d" % s])
                            if first and f == 0:
                                wstage["st32"] = [stg[0], stg[1]]
                                wstage["st16"] = [stg[2].bitcast(BF16)[:, 0:2048], stg[2].bitcast(BF16)[:, 2048:4096]]
                                for _ in range(3):
                                    if wpending:
                                        wpending.pop(0)()
                        for cc in range(2):
                            c = cp * 2 + cc
                            pg, pu = c % 2, 2 + c % 2
                            for k in range(8):
                                P.mm(ps[pg][:, :], wg[s][:, k, cc * 128:(cc + 1) * 128], xnT[:, k, :], start=(k == 0), stop=(k == 7),
                                     r=["wg%d" % s, "xnT"], w=["ps%d" % pg])
                            for k in range(8):
                                P.mm(ps[pu][:, :], wu[s][:, k, cc * 128:(cc + 1) * 128], xnT[:, k, :], start=(k == 0), stop=(k == 7),
                                     r=["wu%d" % s, "xnT"], w=["ps%d" % pu])
                            P.act(sg[c % 2], ps[pg][:, :], AF.Silu, r=["ps%d" % pg], w=["sg%d" % (c % 2)])
                            P.dve(lambda e, c=c, pu=pu: e.tensor_tensor(hT[:, c, :], ps[pu][:, :], sg[c % 2], ALU.mult),
                                  r=["ps%d" % pu, "sg%d" % (c % 2)], w=["hT%d" % c])
                    for t in range(4):
                        tt = tb * 4 + t
                        for dh in range(2):
                            pd = 4 + dh
                            for c in range(NCH):
                                P.mm(ps[pd][:, :], hT[:, c, t * 128:(t + 1) * 128], wd[:, c, dh * 512:(dh + 1) * 512],
                                     start=(c == 0), stop=(c == NCH - 1), r=["hT%d" % c, "wd"], w=["ps%d" % pd])
                            xs_ = xin[t][:, dh * 512:(dh + 1) * 512]
                            P.dve(lambda e, pd=pd, xs_=xs_: e.scalar_tensor_tensor(xs_, ps[pd][:, :], 0.5, xs_, ALU.mult, ALU.add),
                                  r=["ps%d" % pd, "xin%d" % t], w=["xin%d" % t])
                        store_out(xin[t], "xin%d" % t, tt)
                        if make_u:
                            nb = rms_T(xin[t], "xin%d" % t, PK_GM, lambda tt=tt: uT[:, :, tt * 128:(tt + 1) * 128], "uT", W, defer=True)
                            if carry[0] is not None:
                                carry[0]()
                            carry[0] = nb
                if carry[0] is not None:
                    carry[0]()
                    carry[0] = None
                if first and f == 0:
                    while wpending:
                        wpending.pop(0)()
                P.barrier()
                A.top = top0

            def qk_norm(psrc, np_, ncol, gcol_ap, dst, skey, dkey, W):
                s = cnt["n"] % 2
                cnt["n"] += 1
                sq, rs = W["sq"][s], W["rs"][s]
                P.act(sq[0:np_, 0:ncol], psrc, AF.Square, r=[skey], w=["sq%d" % s])
                P.mm(ps[7][0:np_, 0:ncol], blockones[0:np_, 0:np_], sq[0:np_, 0:ncol], r=["sq%d" % s, "cb"], w=["ps7"])
                P.act(rs[0:np_, 0:ncol], ps[7][0:np_, 0:ncol], AF.Ln, r=["ps7", "sm"], w=["rs%d" % s], scale=1.0 / 64, bias=epsb[0:np_, :])
                P.act(rs[0:np_, 0:ncol], rs[0:np_, 0:ncol], AF.Exp, r=["rs%d" % s], w=["rs%d" % s], scale=-0.5)
                dsts = dst if isinstance(dst, list) else [(slice(0, np_), dst)]
                for (psl, d_) in dsts:
                    def v3(ap, d_=d_):
                        return ap if len(d_.shape) == 2 else ap.rearrange("p (a q) -> p a q", a=d_.shape[1])
                    P.dve(lambda e, psl=psl, d_=d_, v3=v3: e.scalar_tensor_tensor(d_, v3(psrc[psl]), gcol_ap[psl], v3(rs[psl, 0:ncol]),
                                                                             ALU.mult, ALU.mult),
                          r=[skey, "rs%d" % s, "sm", "pk"], w=[dkey])

            def run_tiles(tiles):
                def emit_S(tl):
                    sb = cnt["S"] % 3
                    cnt["S"] += 1
                    tl["sb"] = sb
                    n = len(tl["mms"])
                    c0 = tl.get("c0", 0)
                    for q, (l_, r_, kk) in enumerate(tl["mms"]):
                        P.mm(ps[sb][0:tl["kp"], c0:512], l_, r_, start=(q == 0), stop=(q == n - 1), r=kk, w=["ps%d" % sb])

                def emit_E(tl):
                    pslot = cnt["P"] % 3
                    cnt["P"] += 1
                    tl["pt"] = pslot
                    sb = tl["sb"]
                    c0 = tl.get("c0", 0)
                    P.act(Pt[pslot][0:tl["kp"], c0:512], ps[sb][0:tl["kp"], c0:512], AF.Exp, r=["ps%d" % sb], w=["Pt%d" % pslot])

                def emit_PV(tl):
                    pslot = tl["pt"]
                    Ob = tl["Ob"]
                    okey = "ps%d" % Ob
                    oc = tl["ocols"]
                    if tl.get("init"):
                        P.mm(ps[Ob][:, 0:tl["init"] * oc], cb[0:1, CB_E + 128:CB_E + 256], cb[0:1, CB_E + 128:CB_E + 128 + tl["init"] * oc],
                             start=True, stop=False, r=["cb"], w=[okey])
                    npv = len(tl["pv"])
                    for q_, (sub, rhs, sp_, kk) in enumerate(tl["pv"]):
                        last = bool(tl.get("after")) and q_ == npv - 1
                        P.mm(ps[Ob][:, sub * oc:(sub + 1) * oc], Pt[pslot][0:tl["kp"], sub * 128:(sub + 1) * 128], rhs,
                             start=False, stop=last, r=["Pt%d" % pslot] + kk, w=[okey])
                    if tl.get("after"):
                        tl["after"]()

                emit_S(tiles[0])
                for idx, tl in enumerate(tiles):
                    if idx + 1 < len(tiles):
                        emit_S(tiles[idx + 1])
                    emit_E(tl)
                    emit_PV(tl)

            def next_O():
                ob = 3 + cnt["O"] % 3
                cnt["O"] += 1
                return ob

            for b in range(nseq):
                uT = A.bf16(8 * S).rearrange("p (k n) -> p k n", k=8)

                def load_x1(xt, key, tt, b=b):
                    P.dma(xt, x_d[b, tt * 128:(tt + 1) * 128, :], w=[key])

                def store_x1(xt, key, tt, b=b):
                    P.dma(x1_s[b, tt * 128:(tt + 1) * 128, :], xt, r=[key], w=["x1s_%d" % tt], q="pool")

                ffn(0, b, load_x1, store_x1, True, uT)
                check(P, "P1", [("x1", x1_s[0]), ("uT", uT.rearrange("p k n -> p (k n)"))])
                oT = A.bf16(8 * S).rearrange("p (k n) -> p k n", k=8)
                seq_top = A.top

                Pt = [A.bf16(512) for _ in range(3)]
                Wn = {"sq": [A.bf16(512), A.bf16(512)], "rs": [A.f32(512), A.f32(512)]}
                att_top = A.top
                QA = [A.bf16(16 * 512).rearrange("p (i r q) -> p i r q", i=16, r=4) for _ in range(2)]
                KA = [A.bf16(S), A.bf16(S)]
                RA = [slice(0, 96), None]
                RS = [slice(64, 96), slice(0, 32)]
                KTw = A.bf16(S)
                rawk = A.bf16(S)
                rawv = A.bf16(S)
                Vs = A.bf16(16 * 2 * 66).rearrange("p (t g d) -> p t g d", t=16, g=2)
                Vw = A.bf16(16 * 2 * 66).rearrange("p (t g d) -> p t g d", t=16, g=2)
                gn = A.f32(16 * 24).rearrange("p (t c) -> p t c", t=16)
                KTc = A.bf16(128)
                Vc = A.bf16(2 * 98).rearrange("p (g d) -> p g d", g=2)
                top_a = A.top
                wA = A.bf16(8 * 1304).rearrange("p (k n) -> p k n", k=8)
                P.dma(wA, winA_s.rearrange("(k p) n -> p k n", p=128), w=["wA"])
                P.dma(KA[0][64:96, :], cb[0:32, CB_E:CB_E + 2048], r=["cb"], w=["KAe0"])
                P.dma(KA[1][0:32, :], cb[0:32, CB_E:CB_E + 2048], r=["cb"], w=["KAe1"])
                P.pool(lambda e: e.memset(KA[1][32:64, :], 0.0), w=["KAz1"])
                P.pool(lambda e: e.memset(QA[1][32:64, :, :, :].rearrange("p i r q -> p (i r q)"), 0.0), w=["QAz1"])
                P.pool(lambda e: e.memset(Vs.rearrange("p t g d -> p (t g d)"), 1.0), w=["Vs"])
                P.pool(lambda e: e.memset(Vw.rearrange("p t g d -> p (t g d)"), 1.0), w=["Vw"])
                for tb in range(4):
                    ub = lambda k, tb=tb: uT[:, k, tb * 512:(tb + 1) * 512]
                    chunks = [("q", r, r * 128) for r in range(4)] + [("ks", 0, 768), ("kw", 0, 896), ("kc", 0, 512), ("vc", 0, 640)]
                    for ci, (kind, r, col) in enumerate(chunks):
                        pb_ = ci % 2
                        for k in range(8):
                            P.mm(ps[pb_][:, :], wA[:, k, col:col + 128], ub(k), start=(k == 0), stop=(k == 7),
                                 r=["wA", "uT"], w=["ps%d" % pb_])
                        if kind == "q":
                            qk_norm(ps[pb_][:, :], 128, 512, sm[:, 2:3],
                                    [(slice(0, 64), QA[0][0:64, 4 * tb:4 * tb + 4, r, :]), (slice(64, 128), QA[1][64:128, 4 * tb:4 * tb + 4, r, :])],
                                    "ps%d" % pb_, "QTn", Wn)
                        elif kind == "ks":
                            qk_norm(ps[pb_][:, :], 128, 512, pk[:, PK_GKS:PK_GKS + 1],
                                    [(slice(0, 64), KA[0][0:64, tb * 512:(tb + 1) * 512]), (slice(64, 128), KA[1][64:128, tb * 512:(tb + 1) * 512])],
                                    "ps%d" % pb_, "KTs", Wn)
                        elif kind == "kw":
                            qk_norm(ps[pb_][:, :], 128, 512, pk[:, PK_GKW:PK_GKW + 1], KTw[:, tb * 512:(tb + 1) * 512], "ps%d" % pb_, "KTw", Wn)
                        else:
                            dst = (rawk if kind == "kc" else rawv)[:, tb * 512:(tb + 1) * 512]
                            P.act(dst, ps[pb_][:, :], AF.Copy, r=["ps%d" % pb_], w=["raw" + kind])
                for tt in range(16):
                    pb_ = 2 + tt % 2
                    for k in range(8):
                        P.mm(ps[pb_][:, 0:280], uT[:, k, tt * 128:(tt + 1) * 128], wA[:, k, 1024:1304], start=(k == 0), stop=(k == 7),
                             r=["wA", "uT"], w=["ps%d" % pb_])
                    if tt == 0:
                        check(P, "P2a2", [("gn", gn.rearrange("p t c -> p (t c)"))])
                    P.act(Vs[:, tt, :, 0:64], ps[pb_][:, 0:128].rearrange("p (g d) -> p g d", g=2), AF.Copy, r=["ps%d" % pb_, "Vs"], w=["Vs"])
                    if tt == 0:
                        check(P, "P2a3", [("gn", gn.rearrange("p t c -> p (t c)"))])
                    P.act(Vw[:, tt, :, 0:64], ps[pb_][:, 128:256].rearrange("p (g d) -> p g d", g=2), AF.Copy, r=["ps%d" % pb_, "Vw"], w=["Vw"])
                    if tt == 0:
                        check(P, "P2a4", [("gn", gn.rearrange("p t c -> p (t c)"))])
                    P.act(gn[:, tt, :], ps[pb_][:, 256:280], AF.Sigmoid, r=["ps%d" % pb_, "gn"], w=["gn"])
                    if tt == 0:
                        check(P, "P2a5", [("gn", gn.rearrange("p t c -> p (t c)"))])
                P.barrier()
                check(P, "P2a", [("QA0", QA[0].rearrange("p i r q -> p (i r q)")), ("QA1", QA[1].rearrange("p i r q -> p (i r q)")),
                                 ("KA0", KA[0]), ("KA1", KA[1]), ("KTw", KTw), ("rawk", rawk),
                                 ("Vs", Vs.rearrange("p t g d -> p (t g d)")), ("gn", gn.rearrange("p t c -> p (t c)"))])
                A.top = top_a

                w1 = [A.bf16(32 * 128).rearrange("p (l h) -> p l h", l=32) for _ in range(2)]
                w2k = A.bf16(192)
                w2v = A.bf16(64)
                posT = [A.bf16(32), A.bf16(32)]
                hb = [A.f32(1), A.f32(1)]
                hs = [[A.bf16(128) for _ in range(2)] for _ in range(2)]
                for kv in range(2):
                    srcw = cw1_s[kv].rearrange("(l d) h -> d l h", d=64)
                    P.dma(w1[kv][0:64], srcw, w=["w1_%d" % kv])
                    P.dma(w1[kv][64:128], srcw, r=["w1_%d" % kv], w=["w1_%d" % kv])
                    pcol = PK_POSK if kv == 0 else PK_POSV
                    P.dve(lambda e, kv=kv, pcol=pcol: e.tensor_copy(posT[kv], pk[:, pcol:pcol + 32]), r=["pk"], w=["posT%d" % kv])
                P.pool(lambda e: e.memset(w2k, 0.0), w=["w2k"])
                P.dma(w2k[:, 64:128], cw2_s[0], r=["w2k"], w=["w2k"])
                P.dma(w2v, cw2_s[1], w=["w2v"])
                for kv in range(2):
                    for l in range(32):
                        P.mm(ps[0][:, 0:1], w1[kv][0:64, l, :], posT[kv][0:64, l:l + 1], start=(l == 0), stop=(l == 31),
                             r=["w1_%d" % kv, "posT%d" % kv], w=["ps0"])
                    P.act(hb[kv], ps[0][:, 0:1], AF.Copy, r=["ps0"], w=["hb%d" % kv])
                    raw = rawk if kv == 0 else rawv
                    for g in range(2):
                        pb_ = 1 + g
                        for l in range(32):
                            P.mm(ps[pb_][:, 0:127], w1[kv][g * 64:(g + 1) * 64, l, :],
                                 raw.rearrange("p (c s) -> p c s", s=16)[g * 64:(g + 1) * 64, (l // 16):(l // 16) + 127, l % 16],
                                 start=(l == 0), stop=(l == 31), r=["w1_%d" % kv, "rawkc" if kv == 0 else "rawvc"], w=["ps%d" % pb_])
                        P.act(hs[kv][g][:, 0:127], ps[pb_][:, 0:127], AF.Silu, r=["ps%d" % pb_, "hb%d" % kv], w=["hs%d%d" % (kv, g)], bias=hb[kv])
                P.mm(ps[3][:, 0:127], w2k[:, 64:192], hs[0][0][:, 0:127], start=True, stop=False, r=["w2k", "hs00"], w=["ps3"])
                P.mm(ps[3][:, 0:127], w2k[:, 0:128], hs[0][1][:, 0:127], start=False, stop=True, r=["w2k", "hs01"], w=["ps3"])
                qk_norm(ps[3][:, 0:127], 128, 127, pk[:, PK_GKC:PK_GKC + 1], KTc[:, 0:127], "ps3", "KTc", {"sq": [Wn["sq"][0][:, 0:128], Wn["sq"][1][:, 0:128]],
                                                                                                "rs": [Wn["rs"][0][:, 0:128], Wn["rs"][1][:, 0:128]]})
                P.pool(lambda e: e.memset(Vc.rearrange("p g d -> p (g d)"), 1.0), w=["Vc"])
                for g in range(2):
                    P.mm(ps[4 + g][0:127, 0:64], hs[1][g][:, 0:127], w2v, r=["hs1%d" % g, "w2v"], w=["ps%d" % (4 + g)])
                    P.act(Vc[0:127, g, 0:64], ps[4 + g][0:127, 0:64], AF.Copy, r=["ps%d" % (4 + g), "Vc"], w=["Vc"])
                    P.dve(lambda e, g=g: e.tensor_copy(Vc[0:127, g, 65:97], cb[0:127, CB_OV:CB_OV + 32]), r=["cb", "Vc"], w=["Vc"])
                P.barrier()
                check(P, "P3", [("KTc", KTc), ("Vc", Vc.rearrange("p g d -> p (g d)"))])
                A.top = top_a

                otok = [A.bf16(512), A.bf16(512)]
                oacc = [A.f32(256), A.f32(256)]
                tmpS = A.f32(256)
                tmpW = A.f32(256)
                rcs = [[A.f32(4) for _ in range(3)] for _ in range(2)]
                w4s = [[A.f32(4) for _ in range(3)] for _ in range(2)]
                tmp32 = [A.f32(128), A.f32(128)]
                imp = [A.f32(32), A.f32(32)]
                top8 = [A.f32(8), A.f32(8)]
                sel = [A.f32(96), A.f32(96)]
                nselT1 = [A.bf16(512, parts=32), A.bf16(512, parts=32)]
                E_ = lambda j: cb[0:32, CB_E + j * 128:CB_E + (j + 1) * 128]
                steps = [(i, g) for i in range(16) for g in range(2)]
                pending = []

                def Qof(i, g):
                    return QA[g][g * 64:(g + 1) * 64, i, :, :].rearrange("p r q -> p (r q)")

                def norm_gate(Ob, oc, sl, br, i, g):
                    okey = "ps%d" % Ob
                    Ov = ps[Ob][:, 0:4 * oc].rearrange("p (r d) -> p r d", r=4)
                    den = Ov[:, :, 64:65].rearrange("p r d -> p (r d)")
                    rc_, w4_ = rcs[sl][br], w4s[sl][br]
                    kk = "%d%d" % (sl, br)
                    P.dve(lambda e: e.tensor_scalar(rc_, den, 1e-30, None, ALU.max), r=[okey], w=["rc" + kk])
                    P.dve(lambda e: e.reciprocal(rc_, rc_), r=["rc" + kk], w=["rc" + kk])
                    gcol0 = br * 8 + 4 * g
                    P.dve(lambda e: e.tensor_tensor(w4_, rc_, gn[:, i, gcol0:gcol0 + 4], ALU.mult), r=["rc" + kk, "gn"], w=["w4" + kk])
                    return Ov, rc_, w4_.unsqueeze(2).to_broadcast([128, 4, 64]), okey, kk

                def cmp_tile(n):
                    i, g = steps[n]
                    sl = n % 2
                    gs = slice(g * 64, (g + 1) * 64)
                    Ob = next_O()

                    def after():
                        Ov, rc_, w4b, okey, kk = norm_gate(Ob, 97, sl, 0, i, g)
                        oa3 = oacc[sl].rearrange("p (r d) -> p r d", r=4)
                        P.dve(lambda e: e.tensor_tensor(oa3, Ov[:, :, 0:64], w4b, ALU.mult), r=[okey, "w4" + kk], w=["oacc%d" % sl])
                        rcb = rc_.unsqueeze(2).to_broadcast([128, 4, 32])
                        t32, im, t8 = tmp32[sl], imp[sl], top8[sl]
                        se = sel[sl][:, RS[g]]
                        P.dve(lambda e: e.tensor_tensor(t32.rearrange("p (r b) -> p r b", r=4), Ov[:, :, 65:97], rcb, ALU.mult),
                              r=[okey, "rc" + kk], w=["tmp32_%d" % sl])
                        P.dve(lambda e: e.tensor_reduce(im, t32.rearrange("p (r b) -> p b r", r=4), AX.X, ALU.add),
                              r=["tmp32_%d" % sl], w=["imp%d" % sl])
                        P.dve(lambda e: e.tensor_tensor(im, im, cf[:, CF_BONUS + i * 32:CF_BONUS + (i + 1) * 32], ALU.add),
                              r=["imp%d" % sl, "cf"], w=["imp%d" % sl])
                        P.dve(lambda e: e.max(t8, im), r=["imp%d" % sl], w=["top8_%d" % sl])
                        P.dve(lambda e: e.tensor_scalar(se, im, t8[:, 7:8], None, ALU.is_ge), r=["imp%d" % sl, "top8_%d" % sl], w=["sel%d" % sl])
                        P.dve(lambda e: e.tensor_scalar(se, se, -NEG, NEG, ALU.mult, ALU.add), r=["sel%d" % sl], w=["sel%d" % sl])

                    return dict(kp=127, ocols=97, Ob=Ob, init=4, after=after,
                                mms=[(KTc[gs, 0:127], Qof(i, g), ["KTc", "QTn"]),
                                     (cb[0:18, CB_SELC + i * 128:CB_SELC + i * 128 + 127], Gext[g][0:18, :], ["cb", "Gext"])],
                                pv=[(r, Vc[0:127, g, 0:97], True, ["Vc"]) for r in range(4)])

                def cmp_finish(n):
                    i, g = steps[n]
                    sl = n % 2
                    P.tr(ps[7][0:96, 0:128], sel[sl], identf, r=["sel%d" % sl, "cf"], w=["ps7"])
                    dst = QA[g][RS[g], i, :, :]
                    P.dve(lambda e: e.tensor_copy(dst, ps[7][RS[g], 0:128].unsqueeze(1).to_broadcast([32, 4, 128])),
                          r=["ps7"], w=["nsel_%d" % n])

                def main_tiles(n):
                    i, g = steps[n]
                    sl = n % 2
                    gs = slice(g * 64, (g + 1) * 64)
                    Qi = Qof(i, g)
                    tiles = []
                    for br in (1, 2):
                        Ob = next_O()
                        j0 = 0 if br == 1 else max(0, i - 4)
                        KT, V, kkey, vkey = (None, Vs, "KTs", "Vs") if br == 1 else (KTw, Vw, "KTw", "Vw")
                        for j in range(j0, i + 1):
                            dl = i - j
                            if br == 1 and g == 0:
                                mms = [(KA[0][RA[0], j * 128:(j + 1) * 128], QA[0][RA[0], i, :, :].rearrange("p r q -> p (r q)"),
                                        ["KTs", "QTn", "KAe0", "nsel_%d" % n])]
                            elif br == 1:
                                mms = [(KA[1][:, j * 128:(j + 1) * 128], QA[1][:, i, :, :].rearrange("p r q -> p (r q)"),
                                        ["KTs", "QTn", "KAe1", "KAz1", "QAz1", "nsel_%d" % n])]
                            else:
                                mms = [(KT[gs, j * 128:(j + 1) * 128], Qi, [kkey, "QTn"])]
                            if dl == 0:
                                mms.append((ident, Tb["T0", g], ["cb", "Tb"]))
                            elif dl == 1:
                                mms.append((ident, Tb["T128", g], ["cb", "Tb"]))
                            elif dl == 4 and br == 2:
                                mms.append((ident, Tb["Tfar", g], ["cb", "Tb"]))
                            tl = dict(kp=128, ocols=65, Ob=Ob, mms=mms, init=(4 if j == j0 else None),
                                      pv=[(r, V[:, j, g, 0:65], j == i, [vkey]) for r in range(4)])
                            if j == i:
                                def after(br=br, Ob=Ob):
                                    Ov, rc_, w4b, okey, kk = norm_gate(Ob, 65, sl, br, i, g)
                                    tm = tmpS if br == 1 else tmpW
                                    tk = "tmpS" if br == 1 else "tmpW"
                                    P.dve(lambda e: e.tensor_tensor(tm.rearrange("p (r d) -> p r d", r=4), Ov[:, :, 0:64], w4b, ALU.mult),
                                          r=[okey, "w4" + kk], w=[tk])
                                    if br == 1:
                                        P.pool(lambda e: e.tensor_tensor(oacc[sl], oacc[sl], tm, ALU.add), r=["oacc%d" % sl, tk], w=["oacc%d" % sl])
                                    else:
                                        ot = otok[i % 2]
                                        P.pool(lambda e: e.tensor_tensor(ot[:, g * 256:(g + 1) * 256], oacc[sl], tm, ALU.add),
                                               r=["oacc%d" % sl, tk], w=["otok%d_%d" % (i % 2, g)])
                                tl["after"] = after
                            tiles.append(tl)
                    return tiles

                def o_transpose(i):
                    ot = otok[i % 2]
                    pb6 = psb(6)[:, 0:512].rearrange("p (c t) -> p c t", c=4)
                    for c in range(4):
                        P.tr(pb6[:, c, :], ot[:, c * 128:(c + 1) * 128], ident, r=["otok%d_%d" % (i % 2, c // 2), "cb"], w=["ps6"])
                    P.dve(lambda e: e.tensor_copy(oT[:, 0:4, i * 128:(i + 1) * 128], pb6), r=["ps6"], w=["oT"])

                if b == 0:
                    wcfg["late"] = True
                    wstage["st32"] = [A.f32(2048), A.f32(2048)]
                    l16 = A.bf16(4096)
                    wstage["st16"] = [l16[:, 0:2048], l16[:, 2048:4096]]
                run_tiles([cmp_tile(0)])
                cmp_finish(0)
                for n in range(32):
                    i, g = steps[n]
                    tiles = ([cmp_tile(n + 1)] if n + 1 < 32 else []) + main_tiles(n)
                    run_tiles(tiles)
                    if b == 0 and wlate:
                        wlate.pop(0)()
                    for fn in pending:
                        fn()
                    pending = []
                    if n + 1 < 32:
                        cmp_finish(n + 1)
                    if g == 1:
                        pending.append(lambda i=i: o_transpose(i))
                for fn in pending:
                    fn()
                while b == 0 and wlate:
                    wlate.pop(0)()
                P.barrier()
                check(P, "P4a", [("oT", oT.rearrange("p k n -> p (k n)"))])
                A.top = att_top

                wBv = winB_s.rearrange("(k p) n -> p k n", p=128)
                wf = A.bf16(8 * 8).rearrange("p (k n) -> p k n", k=8)
                ee = A.f32(S, parts=8)
                cumn = A.f32(S, parts=8)
                r1 = A.f32(S, parts=8)
                onesf = A.f32(S, parts=8)
                spl = [A.bf16(S, parts=8) for _ in range(6)]
                ones8 = A.bf16(S, parts=8)
                P.pool(lambda e: e.memset(onesf, 1.0), w=["onesf"])
                P.pool(lambda e: e.memset(ones8, 1.0), w=["ones8"])
                P.dma(wf, wBv[:, :, 1536:1544], w=["wf"])
                for tb in range(4):
                    for k in range(8):
                        P.mm(ps[4][0:8, :], wf[:, k, :], uT[:, k, tb * 512:(tb + 1) * 512], start=(k == 0), stop=(k == 7),
                             r=["wf", "uT"], w=["ps4"])
                    P.act(ee[:, tb * 512:(tb + 1) * 512], ps[4][0:8, :], AF.Exp, r=["ps4", "sm", "ee"], w=["ee"], scale=-1.0, bias=sm[0:8, 4:5])
                P.act(ee, ee, AF.Ln, r=["ee", "sm"], w=["ee"], bias=oneb[0:8, :])
                P.dve(lambda e: e.tensor_tensor_scan(cumn, onesf, ee, 0.0, ALU.mult, ALU.add), r=["onesf", "ee"], w=["cumn"])
                P.dve(lambda e: e.tensor_copy(spl[0], cumn), r=["cumn"], w=["spl0"])
                P.dve(lambda e: e.tensor_tensor(r1, cumn, spl[0], ALU.subtract), r=["cumn", "spl0"], w=["r1"])
                P.dve(lambda e: e.tensor_copy(spl[1], r1), r=["r1"], w=["spl1"])
                P.dve(lambda e: e.tensor_tensor(r1, r1, spl[1], ALU.subtract), r=["r1", "spl1"], w=["r1"])
                P.dve(lambda e: e.tensor_copy(spl[2], r1), r=["r1"], w=["spl2"])
                for q in range(3):
                    P.dve(lambda e, q=q: e.tensor_scalar(spl[3 + q], spl[q], -1.0, None, ALU.mult), r=["spl%d" % q], w=["spl%d" % (3 + q)])
                qrows = [spl[3], spl[4], spl[5], ones8, ones8, ones8]
                krows = [ones8, ones8, ones8, spl[0], spl[1], spl[2]]
                allspl = ["spl%d" % q for q in range(6)] + ["ones8"]
                for rr in range(6):
                    P.dma(aug_s[0, rr], qrows[rr], r=allspl, w=["augq%d" % rr])
                    P.dma(aug_s[1, rr], krows[rr], r=allspl, w=["augk%d" % rr])
                P.barrier()
                check(P, "P2b0", [("aug", aug_s.rearrange("a r h n -> (a r h) n"))])
                A.top = att_top

                QTf = A.bf16(8 * S, parts=70).rearrange("p (h n) -> p h n", h=8)
                KTf = A.bf16(8 * S, parts=70).rearrange("p (h n) -> p h n", h=8)
                Vf = A.bf16(16 * 8 * 66).rearrange("p (t h d) -> p t h d", t=16, h=8)
                top_f = A.top
                wq = [A.bf16(8 * 128).rearrange("p (k n) -> p k n", k=8) for _ in range(2)]
                wv = A.bf16(8 * 512).rearrange("p (k n) -> p k n", k=8)
                P.pool(lambda e: e.memset(Vf.rearrange("p t h d -> p (t h d)"), 1.0), w=["Vf"])
                P.dma(QTf[64:70], aug_s[0], r=["augq%d" % rr for rr in range(6)], w=["QTfa"])
                P.dma(KTf[64:70], aug_s[1], r=["augk%d" % rr for rr in range(6)], w=["KTfa"])
                check(P, "P2b1", [("QTf", QTf[64:70].rearrange("p h n -> p (h n)"))])
                for which in range(2):
                    for hp in range(4):
                        s = cnt["w"] % 2
                        cnt["w"] += 1
                        P.dma(wq[s], wBv[:, :, which * 512 + hp * 128:which * 512 + (hp + 1) * 128], w=["wq%d" % s])
                        for hh in range(2):
                            h = hp * 2 + hh
                            for tb in range(4):
                                pb_ = (hh * 4 + tb) % 2
                                for k in range(8):
                                    P.mm(ps[pb_][0:64, :], wq[s][:, k, hh * 64:(hh + 1) * 64], uT[:, k, tb * 512:(tb + 1) * 512],
                                         start=(k == 0), stop=(k == 7), r=["wq%d" % s, "uT"], w=["ps%d" % pb_])
                                if which == 0:
                                    qk_norm(ps[pb_][0:64, :], 64, 512, sm[0:64, 3:4], QTf[0:64, h, tb * 512:(tb + 1) * 512], "ps%d" % pb_, "QTf", Wn)
                                    if h == 0 and tb == 0:
                                        check(P, "P2b2", [("QTf", QTf[0:64, 0, 0:512])])
                                else:
                                    qk_norm(ps[pb_][0:64, :], 64, 512, pk[0:64, PK_GKF:PK_GKF + 1], KTf[0:64, h, tb * 512:(tb + 1) * 512], "ps%d" % pb_, "KTf", Wn)
                P.dma(wv, wBv[:, :, 1024:1536], w=["wv"])
                for tt in range(16):
                    pb_ = 2 + tt % 2
                    for k in range(8):
                        P.mm(ps[pb_][:, :], uT[:, k, tt * 128:(tt + 1) * 128], wv[:, k, :], start=(k == 0), stop=(k == 7),
                             r=["wv", "uT"], w=["ps%d" % pb_])
                    P.act(Vf[:, tt, :, 0:64], ps[pb_][:, :].rearrange("p (h d) -> p h d", h=8), AF.Copy, r=["ps%d" % pb_, "Vf"], w=["Vf"])
                P.barrier()
                check(P, "P2b", [("QTf", QTf.rearrange("p h n -> p (h n)")), ("KTf", KTf.rearrange("p h n -> p (h n)")),
                                 ("Vf", Vf.rearrange("p t h d -> p (t h d)"))])
                A.top = top_f

                oftok = A.bf16(4 * 512).rearrange("p (t n) -> p t n", t=4)
                rcf = [A.f32(4), A.f32(4)]
                for I in range(4):
                    tiles = []
                    for h in range(8):
                        Ob = next_O()
                        Qb = QTf[0:70, h, I * 512:(I + 1) * 512]
                        nj = 4 * I + 4
                        for j in range(nj):
                            c0 = 128 * max(0, j - 4 * I)
                            mms = [(KTf[0:70, h, j * 128:(j + 1) * 128], Qb[:, c0:512], ["KTf", "QTf", "KTfa", "QTfa"])]
                            if j >= 4 * I:
                                m = j - 4 * I
                                mms.append((ident, cb[:, CB_MASK + m * 512 + c0:CB_MASK + (m + 1) * 512], ["cb"]))
                            pv = [(t, Vf[:, j, h, 0:65], j == 4 * I + t, ["Vf"]) for t in range(4) if 4 * I + t >= j]
                            tl = dict(kp=128, ocols=65, Ob=Ob, mms=mms, pv=pv, init=(4 if j == 0 else None), c0=c0)
                            if j == nj - 1:
                                def after(h=h, Ob=Ob):
                                    okey = "ps%d" % Ob
                                    Ov = ps[Ob][:, 0:260].rearrange("p (r d) -> p r d", r=4)
                                    den = Ov[:, :, 64:65].rearrange("p r d -> p (r d)")
                                    rc_ = rcf[h % 2]
                                    kk = "rcf%d" % (h % 2)
                                    P.dve(lambda e: e.tensor_scalar(rc_, den, 1e-30, None, ALU.max), r=[okey], w=[kk])
                                    P.dve(lambda e: e.reciprocal(rc_, rc_), r=[kk], w=[kk])
                                    rcb = rc_.unsqueeze(2).to_broadcast([128, 4, 64])
                                    P.dve(lambda e: e.tensor_tensor(oftok[:, :, h * 64:(h + 1) * 64], Ov[:, :, 0:64], rcb, ALU.mult),
                                          r=[okey, kk], w=["oftok%d" % h])
                                tl["after"] = after
                            tiles.append(tl)
                    run_tiles(tiles)
                    for t in range(4):
                        pb6 = psb(6)[:, 0:512].rearrange("p (c t) -> p c t", c=4)
                        for c in range(4):
                            P.tr(pb6[:, c, :], oftok[:, t, c * 128:(c + 1) * 128], ident, r=["oftok%d" % (2 * c), "oftok%d" % (2 * c + 1), "cb"], w=["ps6"])
                        tt = 4 * I + t
                        P.dve(lambda e, tt=tt, pb6=pb6: e.tensor_copy(oT[:, 4:8, tt * 128:(tt + 1) * 128], pb6), r=["ps6", "oT"], w=["oT"])
                P.barrier()
                check(P, "P4b", [("oT", oT.rearrange("p k n -> p (k n)"))])
                A.top = seq_top

                won = A.bf16(4 * 1024).rearrange("p (c n) -> p c n", c=4)
                wof = A.bf16(4 * 1024).rearrange("p (c n) -> p c n", c=4)
                wo = A.bf16(8 * 1024).rearrange("p (c n) -> p c n", c=8)
                gma = [A.bf16(8 * 128).rearrange("p (k n) -> p k n", k=8) for _ in range(2)]
                gmb = [A.bf16(8 * 128).rearrange("p (k n) -> p k n", k=8) for _ in range(2)]
                mT = A.bf16(8 * 512).rearrange("p (f n) -> p f n", f=8)
                sa = [A.f32(512), A.f32(512)]
                sb_ = [A.f32(512), A.f32(512)]
                m1 = [A.f32(512), A.f32(512)]
                m2 = [A.f32(512), A.f32(512)]
                xr = [A.f32(1024), A.f32(1024)]
                P.dma(won, won_s.rearrange("(c p) n -> p c n", p=128), w=["won"])
                P.dma(wof, wof_s.rearrange("(c p) n -> p c n", p=128), w=["wof"])
                P.dma(wo, wout_s.rearrange("(c p) n -> p c n", p=128), w=["wo"])
                wGv = winG_s.rearrange("(k p) n -> p k n", p=128)
                for tb in range(4):
                    tbs = slice(tb * 512, (tb + 1) * 512)
                    for f in range(8):
                        s = f % 2
                        P.dma(gma[s], wGv[:, :, f * 128:(f + 1) * 128], w=["gma%d" % s])
                        P.dma(gmb[s], wGv[:, :, 1024 + f * 128:1024 + (f + 1) * 128], w=["gmb%d" % s])
                        b0 = s * 4
                        for c in range(4):
                            P.mm(ps[b0][:, :], won[:, c, f * 128:(f + 1) * 128], oT[:, c, tbs], start=(c == 0), stop=(c == 3),
                                 r=["won", "oT"], w=["ps%d" % b0])
                        for c in range(4):
                            P.mm(ps[b0 + 1][:, :], wof[:, c, f * 128:(f + 1) * 128], oT[:, 4 + c, tbs], start=(c == 0), stop=(c == 3),
                                 r=["wof", "oT"], w=["ps%d" % (b0 + 1)])
                        for k in range(8):
                            P.mm(ps[b0 + 2][:, :], gma[s][:, k, :], uT[:, k, tbs], start=(k == 0), stop=(k == 7),
                                 r=["gma%d" % s, "uT"], w=["ps%d" % (b0 + 2)])
                        for k in range(8):
                            P.mm(ps[b0 + 3][:, :], gmb[s][:, k, :], uT[:, k, tbs], start=(k == 0), stop=(k == 7),
                                 r=["gmb%d" % s, "uT"], w=["ps%d" % (b0 + 3)])
                        P.act(sa[s], ps[b0 + 2][:, :], AF.Sigmoid, r=["ps%d" % (b0 + 2)], w=["sa%d" % s])
                        P.act(sb_[s], ps[b0 + 3][:, :], AF.Sigmoid, r=["ps%d" % (b0 + 3)], w=["sb%d" % s])
                        P.dve(lambda e, s=s, b0=b0: e.tensor_tensor(m1[s], ps[b0][:, :], sa[s], ALU.mult), r=["ps%d" % b0, "sa%d" % s], w=["m1%d" % s])
                        P.dve(lambda e, s=s, b0=b0: e.tensor_tensor(m2[s], ps[b0 + 1][:, :], sb_[s], ALU.mult), r=["ps%d" % (b0 + 1), "sb%d" % s], w=["m2%d" % s])
                        P.pool(lambda e, s=s, f=f: e.tensor_tensor(mT[:, f, :], m1[s], m2[s], ALU.add), r=["m1%d" % s, "m2%d" % s], w=["mT%d" % f])
                    for t in range(4):
                        tt = tb * 4 + t
                        xs_ = xr[tt % 2]
                        xk = "xr%d" % (tt % 2)
                        P.dma(xs_, x1_s[b, tt * 128:(tt + 1) * 128, :], r=["x1s_%d" % tt], w=[xk])
                        for dh in range(2):
                            pd = dh
                            for f in range(8):
                                P.mm(ps[pd][:, :], mT[:, f, t * 128:(t + 1) * 128], wo[:, f, dh * 512:(dh + 1) * 512],
                                     start=(f == 0), stop=(f == 7), r=["mT%d" % f, "wo"], w=["ps%d" % pd])
                            xh = xs_[:, dh * 512:(dh + 1) * 512]
                            P.dve(lambda e, pd=pd, xh=xh: e.tensor_tensor(xh, ps[pd][:, :], xh, ALU.add), r=["ps%d" % pd, xk], w=[xk])
                        P.dma(x1_s[b, tt * 128:(tt + 1) * 128, :], xs_, r=[xk], w=["x1s_%d" % tt], q="pool")
                P.barrier()
                check(P, "P5", [("x2", x1_s[0])])
                A.top = base_top

                def load_x2(xt, key, tt, b=b):
                    P.dma(xt, x1_s[b, tt * 128:(tt + 1) * 128, :], r=["x1s_%d" % tt], w=[key])

                def store_y(xt, key, tt, b=b):
                    P.dma(y_d[b, tt * 128:(tt + 1) * 128, :], xt, r=[key], w=["y_%d_%d" % (b, tt)], q="pool")

                ffn(1, b, load_x2, store_y, False, None)
                A.top = base_top

        try:
            body()
        except _Stop:
            pass
        P.barrier()
        P.finalize(st)
    return nc


_NC_CACHE = {}


def kernel(**inputs):
    inp = {k: np.ascontiguousarray(np.asarray(v)) for k, v in inputs.items()}
    n = 8
    nseq = 2
    if "nc" not in _NC_CACHE:
        _NC_CACHE["nc"] = build_nc(nseq)
    nc = _NC_CACHE["nc"]
    cb, cf = _host_consts()
    pk = _pack_params(inp)
    shared = {
        "ffn1_w_up": inp["ffn1_w_up"][0], "ffn2_w_up": inp["ffn2_w_up"][0],
        "ffn1_w_down": inp["ffn1_w_down"][0], "ffn2_w_down": inp["ffn2_w_down"][0],
        "w_in": inp["w_in"][0],
        "cmp_k_w1": inp["cmp_k_w1"][0], "cmp_v_w1": inp["cmp_v_w1"][0],
        "cmp_k_w2": inp["cmp_k_w2"][0], "cmp_v_w2": inp["cmp_v_w2"][0],
        "w_o_nsa": inp["w_o_nsa"][0], "w_o_fox": inp["w_o_fox"][0], "w_out": inp["w_out"][0],
        "cb": cb, "cf": cf, "pk": pk,
    }
    x = inp["x"]
    in_maps = []
    for c in range(n):
        m = dict(shared)
        m["x"] = np.ascontiguousarray(x[c * nseq:(c + 1) * nseq])
        in_maps.append(m)
    res = run_bass_kernel_spmd(nc, in_maps, core_ids=list(range(n)))
    out = np.concatenate([np.asarray(r["y"]) for r in res.results], axis=0)
    return out.astype(np.float32)
```
